# Optimizing a Trainium2 kernel written in Bass

```python
import math
import jax, jax.numpy as jnp
from jax import lax
import numpy as np

D_MODEL = 2048
BATCH = 4
SEQ = 8192
DEPTH = 2

HEAD_DIM = 128
DIL_GROUPS = ((128, 1), (512, 4), (2048, 16))
N_GROUPS = 3
HEADS_PER_GROUP = 4
DIL_WIDTH = N_GROUPS * HEADS_PER_GROUP * HEAD_DIM
DIL_OUT = HEADS_PER_GROUP * HEAD_DIM
SB_HEADS = 8
SB_WIDTH = SB_HEADS * HEAD_DIM
N_BRANCHES = 2
N_IN = 3 * DIL_WIDTH + 3 * SB_WIDTH + N_BRANCHES * D_MODEL
D_FF = 4 * D_MODEL
ROPE_THETA = 10000.0
BLOCK = 128
EPS = 1e-6

kernel_name = "hybrid_dilated_stickbreak_block"


def rms_norm(x, g):
    xf = x.astype(jnp.float32)
    y = xf * lax.rsqrt(jnp.mean(xf * xf, axis=-1, keepdims=True) + EPS)
    return (y * g.astype(jnp.float32)).astype(x.dtype)


def rotary(x):
    s = x.shape[1]
    half = HEAD_DIM // 2
    inv_freq = ROPE_THETA ** (-jnp.arange(half, dtype=jnp.float32) / half)
    ang = jnp.arange(s, dtype=jnp.float32)[:, None] * inv_freq[None, :]
    cos = jnp.cos(ang)[None, :, None, :]
    sin = jnp.sin(ang)[None, :, None, :]
    xf = x.astype(jnp.float32)
    x1, x2 = xf[..., :half], xf[..., half:]
    out = jnp.concatenate([x1 * cos - x2 * sin, x2 * cos + x1 * sin], axis=-1)
    return out.astype(x.dtype)


def dilated_window_attention(q, k, v, window, dilation):
    b, s, h, d = q.shape
    span = window // dilation
    length = s // dilation
    nb = -(-length // BLOCK)
    lp = nb * BLOCK

    def to_sub(t):
        t = t.reshape(b, length, dilation, h, d).transpose(0, 2, 3, 1, 4)
        t = jnp.pad(t, ((0, 0), (0, 0), (0, 0), (0, lp - length), (0, 0)))
        return t.reshape(b, dilation, h, nb, BLOCK, d)

    def with_prev(t):
        prev = jnp.pad(t[:, :, :, :-1], ((0, 0), (0, 0), (0, 0), (1, 0), (0, 0), (0, 0)))
        return jnp.concatenate([prev, t], axis=4)

    qb = to_sub(q)
    kw = with_prev(to_sub(k))
    vw = with_prev(to_sub(v))
    scores = jnp.einsum('brhnqd,brhnkd->brhnqk', qb, kw).astype(jnp.float32) / math.sqrt(d)
    blk = jnp.arange(nb)
    qi = blk[:, None, None] * BLOCK + jnp.arange(BLOCK)[None, :, None]
    ki = (blk[:, None, None] - 1) * BLOCK + jnp.arange(2 * BLOCK)[None, None, :]
    off = qi - ki
    valid = (off >= 0) & (off <= span) & (ki >= 0)
    scores = jnp.where(valid, scores, -jnp.inf)
    m = jnp.max(scores, axis=-1, keepdims=True)
    p = jnp.exp(scores - m)
    l = jnp.sum(p, axis=-1, keepdims=True)
    o = jnp.einsum('brhnqk,brhnkd->brhnqd', (p / l).astype(v.dtype), vw)
    log_den = (m + jnp.log(l))[..., 0]
    o = o.reshape(b, dilation, h, lp, d)[:, :, :, :length]
    o = o.transpose(0, 3, 1, 2, 4).reshape(b, s, h, d)
    log_den = log_den.reshape(b, dilation, h, lp)[..., :length]
    log_den = log_den.transpose(0, 3, 1, 2).reshape(b, s, h)
    return o, log_den


def stick_breaking_attention(q, k, v):
    b, s, h, d = q.shape
    nb = s // BLOCK
    qb = q.reshape(b, nb, BLOCK, h, d).transpose(1, 0, 3, 2, 4)
    kt = k.transpose(0, 2, 1, 3)
    vt = v.transpose(0, 2, 1, 3)
    key_pos = jnp.arange(s)

    def block(args):
        q_blk, start = args
        z = jnp.einsum('bhqd,bhkd->bhqk', q_blk, kt).astype(jnp.float32) / math.sqrt(d)
        qpos = start + jnp.arange(BLOCK)
        mask = key_pos[None, :] < qpos[:, None]
        log_keep = jnp.where(mask, -jax.nn.softplus(z), 0.0)
        between = lax.cumsum(log_keep, axis=3, reverse=True) - log_keep
        a = jnp.where(mask, jnp.exp(jax.nn.log_sigmoid(z) + between), 0.0)
        return jnp.einsum('bhqk,bhkd->bhqd', a.astype(vt.dtype), vt)

    starts = jnp.arange(nb) * BLOCK
    o = lax.map(block, (qb, starts))
    return o.transpose(1, 0, 3, 2, 4).reshape(b, s, h, d)


def setup_inputs(seed: int = 0) -> dict:
    key = jax.random.key(seed)
    ks = jax.random.split(key, 13)
    f32 = jnp.float32

    def nrm(k, shape, fan_in):
        return jax.random.normal(k, shape, f32) * (fan_in ** -0.5)

    def gain(k, shape):
        return 1.0 + 0.02 * jax.random.normal(k, shape, f32)

    return {
        "x": jax.random.normal(ks[0], (BATCH, SEQ, D_MODEL), f32),
        "norm1_g": gain(ks[1], (DEPTH, D_MODEL)),
        "w_in": nrm(ks[2], (DEPTH, D_MODEL, N_IN), D_MODEL),
        "q_norm_g": gain(ks[3], (DEPTH, N_GROUPS, HEAD_DIM)),
        "k_norm_g": gain(ks[4], (DEPTH, N_GROUPS, HEAD_DIM)),
        "w_up_dil": nrm(ks[5], (DEPTH, DIL_OUT, D_MODEL), DIL_OUT),
        "w_up_sb": nrm(ks[6], (DEPTH, SB_WIDTH, D_MODEL), SB_WIDTH),
        "gate_b": 0.01 * jax.random.normal(ks[7], (DEPTH, N_BRANCHES, D_MODEL), f32),
        "w_out": nrm(ks[8], (DEPTH, D_MODEL, D_MODEL), D_MODEL),
        "norm2_g": gain(ks[9], (DEPTH, D_MODEL)),
        "w_ff1": nrm(ks[10], (DEPTH, D_MODEL, D_FF), D_MODEL),
        "w_ff2": nrm(ks[11], (DEPTH, D_FF, D_MODEL), D_FF),
    }


def reference(x, norm1_g, w_in, q_norm_g, k_norm_g, w_up_dil, w_up_sb, gate_b, w_out, norm2_g, w_ff1, w_ff2):
    b, s, _ = x.shape
    cuts = [DIL_WIDTH, 2 * DIL_WIDTH, 3 * DIL_WIDTH,
            3 * DIL_WIDTH + SB_WIDTH, 3 * DIL_WIDTH + 2 * SB_WIDTH, 3 * DIL_WIDTH + 3 * SB_WIDTH]
    for layer in range(DEPTH):
        h = rms_norm(x, norm1_g[layer])
        proj = h @ w_in[layer]
        q_d, k_d, v_d, q_s, k_s, v_s, g_pre = jnp.split(proj, cuts, axis=-1)
        q_d = q_d.reshape(b, s, N_GROUPS, HEADS_PER_GROUP, HEAD_DIM)
        k_d = k_d.reshape(b, s, N_GROUPS, HEADS_PER_GROUP, HEAD_DIM)
        v_d = v_d.reshape(b, s, N_GROUPS, HEADS_PER_GROUP, HEAD_DIM)

        outs, dens = [], []
        for g, (window, dilation) in enumerate(DIL_GROUPS):
            qg = rotary(rms_norm(q_d[:, :, g], q_norm_g[layer, g]))
            kg = rotary(rms_norm(k_d[:, :, g], k_norm_g[layer, g]))
            o_g, den_g = dilated_window_attention(qg, kg, v_d[:, :, g], window, dilation)
            outs.append(o_g)
            dens.append(den_g)
        wts = jax.nn.softmax(jnp.stack(dens, axis=0), axis=0)
        y_dil = jnp.sum(wts[..., None] * jnp.stack(outs, axis=0).astype(jnp.float32), axis=0)
        y_dil = y_dil.astype(x.dtype).reshape(b, s, DIL_OUT)

        y_sb = stick_breaking_attention(
            q_s.reshape(b, s, SB_HEADS, HEAD_DIM),
            k_s.reshape(b, s, SB_HEADS, HEAD_DIM),
            v_s.reshape(b, s, SB_HEADS, HEAD_DIM),
        ).reshape(b, s, SB_WIDTH)

        gates = jax.nn.sigmoid(g_pre.reshape(b, s, N_BRANCHES, D_MODEL) + gate_b[layer])
        mixed = gates[:, :, 0] * (y_dil @ w_up_dil[layer]) + gates[:, :, 1] * (y_sb @ w_up_sb[layer])
        x = x + mixed @ w_out[layer]

        h2 = rms_norm(x, norm2_g[layer])
        x = x + jnp.square(jax.nn.relu(h2 @ w_ff1[layer])) @ w_ff2[layer]
    return x
```

```python
import math
from contextlib import ExitStack
import numpy as np
import ml_dtypes
import concourse.bass as bass
import concourse.mybir as mybir
from concourse.bass_utils import run_bass_kernel_spmd

F32 = mybir.dt.float32
BF16 = mybir.dt.bfloat16
AF = mybir.ActivationFunctionType
ALU = mybir.AluOpType

D = 2048
DEPTH = 2
HD = 128
NG = 3
DIL = (1, 4, 16)
DILW = 1536
SBW = 1024
NIN = 11776
DFF = 8192
EPS = 1e-6
T = 512
NCORES = 8
NEG = -30000.0
ISQ = 1.0 / math.sqrt(128.0)

ENGS = ("pe", "act", "dve", "pool", "sp")


class Buf:
    __slots__ = ("name", "w", "r", "lsem", "ssem")

    def __init__(self, name):
        self.name = name
        self.w = {}
        self.r = {}
        self.lsem = None
        self.ssem = None


class Prog:
    def __init__(self):
        self.streams = {e: [] for e in ENGS}
        self.count = {e: 0 for e in ENGS}
        self.waited = {e: {} for e in ENGS}
        self.dma_count = {}
        self.n_dma_sems = 0

    def buf(self, name=""):
        return Buf(name)

    def _new_dma_sem(self):
        k = ("d", self.n_dma_sems)
        self.n_dma_sems += 1
        self.dma_count[k] = 0
        return k

    def _collect(self, eng, reads, writes):
        need = {}
        for b in reads:
            for k, v in b.w.items():
                if need.get(k, 0) < v:
                    need[k] = v
        for b in writes:
            for k, v in b.w.items():
                if need.get(k, 0) < v:
                    need[k] = v
            for k, v in b.r.items():
                if k == eng:
                    continue
                if need.get(k, 0) < v:
                    need[k] = v
        wd = self.waited[eng]
        out = []
        for k, v in need.items():
            if k == eng and eng in ("pe", "sp"):
                continue
            if wd.get(k, 0) < v:
                wd[k] = v
                out.append((k, v))
        return out

    def op(self, eng, fn, reads=(), writes=()):
        waits = self._collect(eng, reads, writes)
        self.count[eng] += 1
        v = self.count[eng]
        self.streams[eng].append((waits, fn, (eng, 1)))
        for b in reads:
            b.r[eng] = v
        for b in writes:
            b.w[eng] = v

    def dma(self, eng, fn, reads=(), writes=(), sem_of=None):
        waits = self._collect(eng, reads, writes)
        if sem_of is not None:
            b = sem_of
            if b.lsem is None:
                b.lsem = self._new_dma_sem()
            k = b.lsem
        elif writes:
            b = writes[0]
            if b.lsem is None:
                b.lsem = self._new_dma_sem()
            k = b.lsem
        else:
            b = reads[0]
            if b.ssem is None:
                b.ssem = self._new_dma_sem()
            k = b.ssem
        self.dma_count[k] += 1
        v = 16 * self.dma_count[k]
        self.streams[eng].append((waits, fn, (k, 16)))
        for b in reads:
            b.r[k] = v
        for b in writes:
            b.w[k] = v

    def finish(self, eng, bufs):
        waits = self._collect(eng, bufs, bufs)
        self.streams[eng].append((waits, None, None))

    def run(self, nc):
        with ExitStack() as es:
            sems = {}
            for e in ENGS:
                sems[e] = es.enter_context(nc.semaphore("s_" + e))
            for i in range(self.n_dma_sems):
                sems[("d", i)] = es.enter_context(nc.semaphore("sd%d" % i))
            block = es.enter_context(nc.Block())
            streams = self.streams

            def play(engobj, name):
                for waits, fn, inc in streams[name]:
                    for k, v in waits:
                        engobj.wait_ge(sems[k], v)
                    if fn is not None:
                        fn(engobj).then_inc(sems[inc[0]], inc[1])

            @block.tensor
            def _(e):
                play(e, "pe")

            @block.scalar
            def _(e):
                play(e, "act")

            @block.vector
            def _(e):
                play(e, "dve")

            @block.gpsimd
            def _(e):
                play(e, "pool")

            @block.sync
            def _(e):
                play(e, "sp")


class Ring:
    def __init__(self, K, es, alloc, name, shape, dt, n):
        self.tiles = [es.enter_context(alloc(name + str(i), shape, dt)) for i in range(n)]
        self.bufs = [K.P.buf(name + str(i)) for i in range(n)]
        self.i = 0

    def next(self):
        j = self.i % len(self.tiles)
        self.i += 1
        return self.tiles[j], self.bufs[j]


C_ONES = 0
C_UNEG = 128
C_NONES = 256
C_IDENT = 384
C_NEGW = 512
C_NEGD = 1408
NCBF = 1664
F_ONES = 0
F_PERM = 128
F_INV = 256
NCF32 = 257


def make_consts():
    cb = np.zeros((128, NCBF), np.float32)
    j = np.arange(128)[:, None]
    s = np.arange(128)[None, :]
    cb[:, C_ONES:C_ONES + 128] = 1.0
    cb[:, C_UNEG:C_UNEG + 128] = np.where(j >= s, -1.0, 0.0)
    cb[:, C_NONES:C_NONES + 128] = -1.0
    cb[:, C_IDENT:C_IDENT + 128] = np.eye(128)
    c = np.arange(896)[None, :]
    cb[:, C_NEGW:C_NEGW + 896] = np.where((c - 384) <= j, NEG, 0.0)
    cb[:, C_NEGD:C_NEGD + 128] = np.where(j >= s, 0.0, NEG)
    cb[:, C_NEGD + 128:C_NEGD + 256] = np.where(j <= s, 0.0, NEG)
    cf = np.zeros((128, NCF32), np.float32)
    cf[:, F_ONES:F_ONES + 128] = 1.0
    cf[:, F_PERM:F_PERM + 128] = (j == ((s + 64) % 128)).astype(np.float32)
    cf[:, F_INV] = 1.0 / 128.0
    return cb.astype(ml_dtypes.bfloat16), cf


def rope_tables(S):
    half = HD // 2
    inv_freq = (np.float32(10000.0) ** (-np.arange(half, dtype=np.float32) / np.float32(half))).astype(np.float32)
    ang = (np.arange(S, dtype=np.float32)[:, None] * inv_freq[None, :]).astype(np.float32)
    cos = np.cos(ang).astype(np.float32)
    sin = np.sin(ang).astype(np.float32)
    cosT = np.concatenate([cos, cos], axis=1).T
    sinT = np.concatenate([-sin, sin], axis=1).T
    return np.ascontiguousarray(cosT), np.ascontiguousarray(sinT)


class KB:
    def __init__(self, nc, es):
        self.nc = nc
        self.es = es
        self.P = Prog()
        self.uid = 0

    def sb(self, name, shape, dt):
        return self.es.enter_context(self.nc.sbuf_tensor(name, shape, dt))

    def ps(self, name, shape, dt=F32):
        return self.es.enter_context(self.nc.psum_tensor(name, shape, dt))

    def dram(self, name, shape, dt, kind="Internal"):
        return self.nc.dram_tensor(name, shape, dt, kind=kind).ap()

    def ring(self, name, shape, dt, n, psum=False):
        return Ring(self, self.es, self.nc.psum_tensor if psum else self.nc.sbuf_tensor, name, shape, dt, n)

    def load_consts(self, cbf_ap, cf32_ap):
        self.cbf = self.sb("cbf_sb", [128, NCBF], BF16)
        self.cf = self.sb("cf32_sb", [128, NCF32], F32)
        self.B_c = self.P.buf("consts")
        self.P.dma("sp", lambda e: e.dma_start(out=self.cbf[:], in_=cbf_ap), writes=[self.B_c])
        self.P.dma("sp", lambda e: e.dma_start(out=self.cf[:], in_=cf32_ap), writes=[self.B_c])


class WPrep:
    def __init__(self, K, name, w_ap, Kdim, N, kcb, ncols):
        self.kcb, self.ncols = kcb, ncols
        self.nkb = Kdim // (128 * kcb)
        self.ncb = N // ncols
        self.scr = K.dram("wscr_" + name, [self.nkb, self.ncb, 128, kcb * ncols], BF16)
        self.bufs = {}
        wv = w_ap.rearrange("(kc p) n -> p kc n", p=128)
        b = K.P.buf("w_%s" % name)
        for kb in range(self.nkb):
            for cb in range(self.ncb):
                self.bufs[(kb, cb)] = b
                dst = self.scr[kb, cb].rearrange("p (kc n) -> p kc n", kc=kcb)
                src = wv[:, kb * kcb:(kb + 1) * kcb, cb * ncols:(cb + 1) * ncols]
                K.P.dma("pool", lambda e, dst=dst, src=src: e.dma_start(out=dst, in_=src), writes=[b])


class WStream:
    def __init__(self, K, ring, depth):
        self.K, self.ring, self.depth = K, ring, depth
        self.plan = []
        self.loaded = []
        self.emitted = 0

    def add(self, wp, kb, cb):
        self.plan.append((wp, kb, cb))
        return len(self.plan) - 1

    def get(self, j, depth=None):
        hi = min(len(self.plan), j + (self.depth if depth is None else depth) + 1)
        while self.emitted < hi:
            wp, kb, cb = self.plan[self.emitted]
            tile, b = self.ring.next()
            src = wp.scr[kb, cb]
            self.K.P.dma("sp", lambda e, tile=tile, src=src: e.dma_start(out=tile[:], in_=src),
                         reads=[wp.bufs[(kb, cb)]], writes=[b])
            self.loaded.append((tile, b, wp))
            self.emitted += 1
        tile, b, wp = self.loaded[j]
        return tile[:].rearrange("p (kc n) -> p kc n", kc=wp.kcb), b


def rstd_from_ss(K, ss_ps, B_ss, out_tile, B_out, tmp, B_tmp, inv_n):
    K.P.op("act", lambda e: e.activation(out=tmp, in_=ss_ps, func=AF.Ln, bias=EPS, scale=inv_n),
           reads=[B_ss], writes=[B_tmp])
    K.P.op("act", lambda e: e.activation(out=out_tile, in_=tmp, func=AF.Exp, scale=-0.5),
           reads=[B_tmp], writes=[B_out])


def phase_A(K, TOK, xT, n1g, w_in, qg, kg, gb, cosT, sinT, outs, wring, st_eng="pool"):
    P = K.P
    nt = TOK // T
    wp = WPrep(K, "in%d" % K.uid, w_in, D, NIN, 16, 512)
    K.uid += 1
    ws = WStream(K, wring, 2)
    order = [0, 1, 2, 3, 4, 5, 9, 10, 11, 12, 15, 16, 17, 18, 19, 20, 21, 22, 6, 7, 8, 13, 14]
    for t in range(nt):
        for cb in order:
            ws.add(wp, 0, cb)
    par = K.sb("parA%d" % K.uid, [128, 16 + 3 + 3 + 32], F32)
    B_par = P.buf("parA")
    P.dma("sp", lambda e: e.dma_start(out=par[:, 0:16], in_=n1g), writes=[B_par])
    P.dma("sp", lambda e: e.dma_start(out=par[:, 16:19], in_=qg), writes=[B_par])
    P.dma("sp", lambda e: e.dma_start(out=par[:, 19:22], in_=kg), writes=[B_par])
    P.dma("sp", lambda e: e.dma_start(out=par[:, 22:54], in_=gb), writes=[B_par])

    hT = K.sb("hT%d" % K.uid, [128, 16, T], BF16)
    B_h = [P.buf("h%d" % i) for i in range(16)]
    xr = K.ring("xr%d" % K.uid, [128, T], F32, 3)
    sqr = K.ring("sqr%d" % K.uid, [128, T], BF16, 3)
    rstd = K.sb("rstd%d" % K.uid, [128, T], F32); B_rstd = P.buf("rstd")
    lnt = K.sb("lnt%d" % K.uid, [128, T], F32); B_lnt = P.buf("lnt")
    rstdT = K.sb("rstdT%d" % K.uid, [128, 4], F32); B_rstdT = P.buf("rstdT")
    cs = K.sb("cs%d" % K.uid, [128, 2, T], F32); B_cs = P.buf("cs")
    acc = K.ring("accA%d" % K.uid, [128, T], F32, 3, psum=True)
    ss_ps = K.ps("ssA%d" % K.uid, [128, T]); B_ss = P.buf("ssA")
    s2_ps = K.ps("s2A%d" % K.uid, [128, T]); B_s2 = P.buf("s2A")
    pm_ps = K.ps("pmA%d" % K.uid, [128, T]); B_pm = P.buf("pmA")
    rt_ps = K.ps("rtA%d" % K.uid, [128, 4]); B_rt = P.buf("rtA")
    tr = K.ring("tA%d" % K.uid, [128, T], F32, 2)
    sq2 = K.ring("sq2A%d" % K.uid, [128, T], F32, 4)
    l2 = K.ring("l2A%d" % K.uid, [128, T], F32, 2)
    tn = K.ring("tnA%d" % K.uid, [128, T], F32, 2)
    ra = K.ring("raA%d" % K.uid, [128, T], F32, 2)
    rb = K.ring("rbA%d" % K.uid, [128, T], F32, 2)
    ob = K.ring("obA%d" % K.uid, [128, T], BF16, 4)
    K.uid += 1
    cbf, cf = K.cbf, K.cf
    B_c = K.B_c
    wi = 0
    for t in range(nt):
        t0 = t * T
        P.dma("sp", lambda e, t0=t0: e.dma_start(out=cs[:, 0, :], in_=cosT[:, t0:t0 + T]), writes=[B_cs])
        P.dma("sp", lambda e, t0=t0: e.dma_start(out=cs[:, 1, :], in_=sinT[:, t0:t0 + T]), writes=[B_cs])
        for kc in range(16):
            xt, B_x = xr.next()
            P.dma("sp", lambda e, xt=xt, kc=kc, t0=t0: e.dma_start(out=xt[:], in_=xT[kc * 128:(kc + 1) * 128, t0:t0 + T]),
                  writes=[B_x])
            sq, B_sq = sqr.next()
            P.op("act", lambda e, sq=sq, xt=xt: e.activation(out=sq[:], in_=xt[:], func=AF.Square), reads=[B_x], writes=[B_sq])
            P.op("dve", lambda e, xt=xt, kc=kc: e.tensor_scalar(out=hT[:, kc, :], in0=xt[:], scalar1=par[:, kc:kc + 1], scalar2=None,
                                                               op0=ALU.mult), reads=[B_x, B_par], writes=[B_h[kc]])
            P.op("pe", lambda e, sq=sq, kc=kc: e.matmul(ss_ps[:], lhsT=cbf[:, C_ONES:C_ONES + 128], rhs=sq[:],
                                                        start=(kc == 0), stop=(kc == 15)), reads=[B_sq, B_c], writes=[B_ss])
        rstd_from_ss(K, ss_ps[:], B_ss, rstd[:], B_rstd, lnt[:], B_lnt, 1.0 / D)
        for j in range(4):
            P.op("pe", lambda e, j=j: e.matmul(rt_ps[:, j:j + 1], lhsT=rstd[:, j * 128:(j + 1) * 128], rhs=cf[:, F_INV:F_INV + 1],
                                               start=True, stop=True), reads=[B_rstd, B_c], writes=[B_rt])
        P.op("dve", lambda e: e.tensor_copy(out=rstdT[:], in_=rt_ps[:]), reads=[B_rt], writes=[B_rstdT])

        for cb in order:
            wt, B_w = ws.get(wi)
            wi += 1
            if cb in (6, 7, 8, 13, 14):
                dst = outs["vd"] if cb < 9 else outs["vs"]
                c0 = (cb - 6) * 512 if cb < 9 else (cb - 13) * 512
                for j in range(4):
                    ps, B_ps = acc.next()
                    for kc in range(16):
                        P.op("pe", lambda e, ps=ps, wt=wt, kc=kc, j=j: e.matmul(ps[:], lhsT=hT[:, kc, j * 128:(j + 1) * 128], rhs=wt[:, kc, :],
                                                                                 start=(kc == 0), stop=(kc == 15)),
                             reads=[B_h[kc], B_w], writes=[B_ps])
                    o, B_o = ob.next()
                    P.op("act", lambda e, o=o, ps=ps, j=j: e.activation(out=o[:], in_=ps[:], func=AF.Identity, scale=rstdT[:, j:j + 1]),
                         reads=[B_ps, B_rstdT], writes=[B_o])
                    P.dma(st_eng, lambda e, o=o, dst=dst, j=j, c0=c0, t0=t0: e.dma_start(out=dst[t0 + j * 128:t0 + (j + 1) * 128, c0:c0 + 512], in_=o[:]),
                          reads=[B_o])
                continue
            for ci in range(4):
                ch = cb * 4 + ci
                ps, B_ps = acc.next()
                for kc in range(16):
                    P.op("pe", lambda e, ps=ps, wt=wt, kc=kc, ci=ci: e.matmul(ps[:], lhsT=wt[:, kc, ci * 128:(ci + 1) * 128], rhs=hT[:, kc, :],
                                                                               start=(kc == 0), stop=(kc == 15)),
                         reads=[B_h[kc], B_w], writes=[B_ps])
                o, B_o = ob.next()
                if ch < 24:
                    isq = ch < 12
                    hidx = ch if isq else ch - 12
                    g = hidx // 4
                    gcol = (16 if isq else 19) + g
                    c0 = ISQ if isq else 1.0
                    dst = outs["qdT"] if isq else outs["kdT"]
                    tt, B_t = tr.next()
                    P.op("dve", lambda e, tt=tt, ps=ps: e.tensor_tensor(out=tt[:], in0=ps[:], in1=rstd[:], op=ALU.mult),
                         reads=[B_ps, B_rstd], writes=[B_t])
                    s2, B_q2 = sq2.next()
                    P.op("act", lambda e, s2=s2, tt=tt: e.activation(out=s2[:], in_=tt[:], func=AF.Square), reads=[B_t], writes=[B_q2])
                    P.op("pe", lambda e, s2=s2: e.matmul(s2_ps[:], lhsT=cf[:, F_ONES:F_ONES + 128], rhs=s2[:], start=True, stop=True),
                         reads=[B_q2, B_c], writes=[B_s2])
                    ll, B_l = l2.next()
                    r2, B_r2 = sq2.next()
                    rstd_from_ss(K, s2_ps[:], B_s2, r2[:], B_r2, ll[:], B_l, 1.0 / HD)
                    nn, B_n = tn.next()
                    P.op("dve", lambda e, nn=nn, tt=tt, r2=r2, gcol=gcol: e.scalar_tensor_tensor(out=nn[:], in0=tt[:], scalar=par[:, gcol:gcol + 1],
                                                                                                 in1=r2[:], op0=ALU.mult, op1=ALU.mult),
                         reads=[B_t, B_r2, B_par], writes=[B_n])
                    P.op("pe", lambda e, nn=nn: e.matmul(pm_ps[:], lhsT=cf[:, F_PERM:F_PERM + 128], rhs=nn[:], start=True, stop=True),
                         reads=[B_n, B_c], writes=[B_pm])
                    aa, B_a = ra.next()
                    P.op("dve", lambda e, aa=aa, nn=nn, c0=c0: e.scalar_tensor_tensor(out=aa[:], in0=nn[:], scalar=c0, in1=cs[:, 0, :],
                                                                                      op0=ALU.mult, op1=ALU.mult),
                         reads=[B_n, B_cs], writes=[B_a])
                    bb, B_b = rb.next()
                    P.op("dve", lambda e, bb=bb, c0=c0: e.scalar_tensor_tensor(out=bb[:], in0=pm_ps[:], scalar=c0, in1=cs[:, 1, :],
                                                                               op0=ALU.mult, op1=ALU.mult),
                         reads=[B_pm, B_cs], writes=[B_b])
                    P.op("dve", lambda e, o=o, aa=aa, bb=bb: e.tensor_tensor(out=o[:], in0=aa[:], in1=bb[:], op=ALU.add),
                         reads=[B_a, B_b], writes=[B_o])
                    P.dma(st_eng, lambda e, o=o, dst=dst, hidx=hidx, t0=t0: e.dma_start(out=dst[hidx, :, t0:t0 + T], in_=o[:]), reads=[B_o])
                elif ch < 60:
                    isq = ch < 44
                    hidx = ch - 36 if isq else ch - 44
                    dst = outs["qsT"] if isq else outs["ksT"]
                    c0 = ISQ if isq else 1.0
                    P.op("dve", lambda e, o=o, ps=ps, c0=c0: e.scalar_tensor_tensor(out=o[:], in0=ps[:], scalar=c0, in1=rstd[:],
                                                                                    op0=ALU.mult, op1=ALU.mult),
                         reads=[B_ps, B_rstd], writes=[B_o])
                    P.dma(st_eng, lambda e, o=o, dst=dst, hidx=hidx, t0=t0: e.dma_start(out=dst[hidx, :, t0:t0 + T], in_=o[:]), reads=[B_o])
                else:
                    gi = ch - 60
                    tt, B_t = tr.next()
                    P.op("dve", lambda e, tt=tt, ps=ps: e.tensor_tensor(out=tt[:], in0=ps[:], in1=rstd[:], op=ALU.mult),
                         reads=[B_ps, B_rstd], writes=[B_t])
                    P.op("act", lambda e, o=o, tt=tt, gi=gi: e.activation(out=o[:], in_=tt[:], func=AF.Sigmoid, bias=par[:, 22 + gi:23 + gi], scale=1.0),
                         reads=[B_t, B_par], writes=[B_o])
                    P.dma(st_eng, lambda e, o=o, gi=gi, t0=t0: e.dma_start(out=outs["gT"][gi, :, t0:t0 + T], in_=o[:]), reads=[B_o])
    return ob.bufs


def phase_B(K, S, ins, ydT, ysT, st_eng="sp"):
    P = K.P
    NB = S // 128
    cbf = K.cbf
    B_c = K.B_c
    HS = []
    for i in range(2):
        q = K.sb("hsq%d" % i, [128, S], BF16)
        k = K.sb("hsk%d" % i, [128, S], BF16)
        v = K.sb("hsv%d" % i, [128, NB, 128], BF16)
        HS.append((q, k, v, P.buf("hs%d" % i)))
    accO = K.sb("accO", [128, S], F32); B_accO = P.buf("accO")
    accD = K.sb("accD", [128, S], F32); B_accD = P.buf("accD")
    heads = []
    for sl in range(2):
        for g in range(NG):
            heads.append(("d", g, sl))
    for h in range(4):
        heads.append(("s", h, 0))

    def load_head(i):
        kind, a, b = heads[i]
        q, k, v, B = HS[i % 2]
        if kind == "d":
            hi = a * 2 + b
            r = DIL[a]
            nb = NB // r
            P.dma("sp", lambda e: e.dma_start(out=q[:], in_=ins["qdT"][hi]), writes=[B])
            P.dma("sp", lambda e: e.dma_start(out=k[:], in_=ins["kdT"][hi]), writes=[B])
            vsrc = ins["vd"][:, hi * 128:(hi + 1) * 128].rearrange("(n i c) d -> i c n d", i=128, c=r)
            for c in range(r):
                P.dma("sp", lambda e, c=c: e.dma_start(out=v[:, c * nb:(c + 1) * nb, :], in_=vsrc[:, c]), writes=[B])
        else:
            P.dma("sp", lambda e: e.dma_start(out=q[:], in_=ins["qsT"][a]), writes=[B])
            P.dma("sp", lambda e: e.dma_start(out=k[:], in_=ins["ksT"][a]), writes=[B])
            vsrc = ins["vs"][:, a * 128:(a + 1) * 128].rearrange("(n i) d -> i n d", i=128)
            P.dma("sp", lambda e: e.dma_start(out=v[:], in_=vsrc), writes=[B])

    p1 = K.ring("p1", [128, 512], F32, 2, psum=True)
    p2 = K.ring("p2", [128, 512], F32, 2, psum=True)
    po = K.ring("po", [128, 512], F32, 2, psum=True)
    pd = K.ring("pd", [128, 512], F32, 2, psum=True)
    er = K.ring("er", [128, 512], F32, 2)
    spr = K.ring("spr", [128, 512], BF16, 3)
    lsum = K.sb("lsum", [128, 512], F32); B_lsum = P.buf("lsum")
    lbr = K.ring("lbr", [128, 512], BF16, 2)
    ar = K.ring("ar", [128, 512], BF16, 3)
    yo = K.ring("yo", [128, 512], BF16, 2)
    rcp = K.ring("rcp", [128, 512], F32, 2)

    def run_head(hi_, kind, a, b, q, k, v, B_hs):
        if kind == "d":
            g, sl = a, b
            r = DIL[g]
            nb = NB // r
            for c in range(r):
                for n in range(nb):
                    def sl_(nn, c=c, r=r):
                        st = c + r * 128 * nn
                        return slice(st, st + 127 * r + 1, r)
                    qs = sl_(n)
                    ps, B_ps = p1.next()
                    lo = 0 if n > 0 else 128
                    if n > 0:
                        P.op("pe", lambda e, ps=ps, n=n, qs=qs, sl_=sl_: e.matmul(ps[:, 0:128], lhsT=k[:, sl_(n - 1)], rhs=q[:, qs], start=True, stop=False),
                             reads=[B_hs], writes=[B_ps])
                        P.op("pe", lambda e, ps=ps: e.matmul(ps[:, 0:128], lhsT=cbf[:, C_IDENT:C_IDENT + 128], rhs=cbf[:, C_NEGD:C_NEGD + 128],
                                                             start=False, stop=True), reads=[B_c], writes=[B_ps])
                    P.op("pe", lambda e, ps=ps, qs=qs: e.matmul(ps[:, 128:256], lhsT=k[:, qs], rhs=q[:, qs], start=True, stop=False),
                         reads=[B_hs], writes=[B_ps])
                    P.op("pe", lambda e, ps=ps: e.matmul(ps[:, 128:256], lhsT=cbf[:, C_IDENT:C_IDENT + 128], rhs=cbf[:, C_NEGD + 128:C_NEGD + 256],
                                                         start=False, stop=True), reads=[B_c], writes=[B_ps])
                    pt, B_pt = ar.next()
                    P.op("act", lambda e, pt=pt, ps=ps, lo=lo: e.activation(out=pt[:, lo:256], in_=ps[:, lo:256], func=AF.Exp),
                         reads=[B_ps], writes=[B_pt])
                    o_ps, B_o = po.next()
                    d_ps, B_d = pd.next()
                    ti = c * nb + n
                    if n > 0:
                        P.op("pe", lambda e, o_ps=o_ps, pt=pt, ti=ti: e.matmul(o_ps[:, 0:128], lhsT=v[:, ti - 1, :], rhs=pt[:, 0:128], start=True, stop=False),
                             reads=[B_hs, B_pt], writes=[B_o])
                    P.op("pe", lambda e, o_ps=o_ps, pt=pt, ti=ti, n=n: e.matmul(o_ps[:, 0:128], lhsT=v[:, ti, :], rhs=pt[:, 128:256], start=(n == 0), stop=True),
                         reads=[B_hs, B_pt], writes=[B_o])
                    if n > 0:
                        P.op("pe", lambda e, d_ps=d_ps, pt=pt: e.matmul(d_ps[:, 0:128], lhsT=cbf[:, C_ONES:C_ONES + 128], rhs=pt[:, 0:128], start=True, stop=False),
                             reads=[B_c, B_pt], writes=[B_d])
                    P.op("pe", lambda e, d_ps=d_ps, pt=pt, n=n: e.matmul(d_ps[:, 0:128], lhsT=cbf[:, C_ONES:C_ONES + 128], rhs=pt[:, 128:256], start=(n == 0), stop=True),
                         reads=[B_c, B_pt], writes=[B_d])
                    if g == 0:
                        P.op("dve", lambda e, o_ps=o_ps, qs=qs: e.tensor_copy(out=accO[:, qs], in_=o_ps[:, 0:128]), reads=[B_o], writes=[B_accO])
                        P.op("dve", lambda e, d_ps=d_ps, qs=qs: e.tensor_copy(out=accD[:, qs], in_=d_ps[:, 0:128]), reads=[B_d], writes=[B_accD])
                    else:
                        P.op("dve", lambda e, o_ps=o_ps, qs=qs: e.tensor_tensor(out=accO[:, qs], in0=o_ps[:, 0:128], in1=accO[:, qs], op=ALU.add),
                             reads=[B_o, B_accO], writes=[B_accO])
                        P.op("dve", lambda e, d_ps=d_ps, qs=qs: e.tensor_tensor(out=accD[:, qs], in0=d_ps[:, 0:128], in1=accD[:, qs], op=ALU.add),
                             reads=[B_d, B_accD], writes=[B_accD])
            if g == NG - 1:
                for j in range(S // 512):
                    rc, B_rc = rcp.next()
                    P.op("dve", lambda e, rc=rc, j=j: e.reciprocal(out=rc[:], in_=accD[:, j * 512:(j + 1) * 512]), reads=[B_accD], writes=[B_rc])
                    y, B_y = yo.next()
                    P.op("dve", lambda e, y=y, rc=rc, j=j: e.tensor_tensor(out=y[:], in0=accO[:, j * 512:(j + 1) * 512], in1=rc[:], op=ALU.mult),
                         reads=[B_accO, B_rc], writes=[B_y])
                    P.dma(st_eng, lambda e, y=y, j=j, sl=sl: e.dma_start(out=ydT[sl * 128:(sl + 1) * 128, j * 512:(j + 1) * 512], in_=y[:]), reads=[B_y])
        else:
            h = a
            steps = []
            for qg_ in range(S // 512):
                for kb in range(4 * qg_ + 3, -1, -1):
                    steps.append((qg_, kb))
            n = len(steps)
            st = [dict() for _ in range(n)]
            cur_o = [None]

            def mask_mm(e, ps, o):
                return e.matmul(ps[:], lhsT=cbf[:, C_IDENT:C_IDENT + 128], rhs=cbf[:, C_NEGW + 384 - 128 * o:C_NEGW + 896 - 128 * o],
                                start=False, stop=True)

            def S0(i):
                qg_, kb = steps[i]
                o = kb - 4 * qg_
                ps, B_ps = p1.next()
                st[i]["p1"] = (ps, B_ps)
                P.op("pe", lambda e: e.matmul(ps[:], lhsT=k[:, kb * 128:(kb + 1) * 128], rhs=q[:, qg_ * 512:(qg_ + 1) * 512], start=True, stop=(o < 0)),
                     reads=[B_hs], writes=[B_ps])
                if o >= 0:
                    P.op("pe", lambda e: mask_mm(e, ps, o), reads=[B_c], writes=[B_ps])

            def S1(i):
                ps, B_ps = st[i]["p1"]
                ee, B_e = er.next()
                P.op("act", lambda e: e.activation(out=ee[:], in_=ps[:], func=AF.Exp), reads=[B_ps], writes=[B_e])
                sp, B_sp = spr.next()
                st[i]["sp"] = (sp, B_sp)
                P.op("act", lambda e: e.activation(out=sp[:], in_=ee[:], func=AF.Ln, bias=1.0, scale=1.0), reads=[B_e], writes=[B_sp])

            def S2(i):
                qg_, kb = steps[i]
                o = kb - 4 * qg_
                first = (kb == 4 * qg_ + 3)
                sp, B_sp = st[i]["sp"]
                ps, B_ps = p2.next()
                st[i]["p2"] = (ps, B_ps)
                P.op("pe", lambda e: e.matmul(ps[:], lhsT=k[:, kb * 128:(kb + 1) * 128], rhs=q[:, qg_ * 512:(qg_ + 1) * 512], start=True, stop=False),
                     reads=[B_hs], writes=[B_ps])
                if not first:
                    lb, B_lb = st[i]["lb"]
                    P.op("pe", lambda e: e.matmul(ps[:], lhsT=cbf[:, C_NONES:C_NONES + 128], rhs=lb[:], start=False, stop=False),
                         reads=[B_c, B_lb], writes=[B_ps])
                P.op("pe", lambda e: e.matmul(ps[:], lhsT=cbf[:, C_UNEG:C_UNEG + 128], rhs=sp[:], start=False, stop=(o < 0)),
                     reads=[B_c, B_sp], writes=[B_ps])
                if o >= 0:
                    P.op("pe", lambda e: mask_mm(e, ps, o), reads=[B_c], writes=[B_ps])
                if kb > 0:
                    if first:
                        P.op("pool", lambda e: e.tensor_copy(out=lsum[:], in_=sp[:]), reads=[B_sp], writes=[B_lsum])
                    else:
                        P.op("pool", lambda e: e.tensor_tensor(out=lsum[:], in0=lsum[:], in1=sp[:], op=ALU.add), reads=[B_sp, B_lsum], writes=[B_lsum])
                    lb2, B_lb2 = lbr.next()
                    P.op("pool", lambda e: e.tensor_copy(out=lb2[:], in_=lsum[:]), reads=[B_lsum], writes=[B_lb2])
                    st[i + 1]["lb"] = (lb2, B_lb2)

            def S3(i):
                ps, B_ps = st[i]["p2"]
                aa, B_a = ar.next()
                st[i]["a"] = (aa, B_a)
                P.op("act", lambda e: e.activation(out=aa[:], in_=ps[:], func=AF.Exp), reads=[B_ps], writes=[B_a])

            def S4(i):
                qg_, kb = steps[i]
                first = (kb == 4 * qg_ + 3)
                aa, B_a = st[i]["a"]
                if first:
                    cur_o[0] = po.next()
                o_ps, B_o = cur_o[0]
                P.op("pe", lambda e: e.matmul(o_ps[:], lhsT=v[:, kb, :], rhs=aa[:], start=first, stop=(kb == 0)),
                     reads=[B_hs, B_a], writes=[B_o])
                if kb == 0:
                    y, B_y = yo.next()
                    P.op("dve", lambda e: e.tensor_copy(out=y[:], in_=o_ps[:]), reads=[B_o], writes=[B_y])
                    P.dma(st_eng, lambda e: e.dma_start(out=ysT[h * 128:(h + 1) * 128, qg_ * 512:(qg_ + 1) * 512], in_=y[:]), reads=[B_y])
                st[i].clear()

            stages = [S0, S1, S2, S3, S4]
            for tick in range(n + 4):
                for si, fn in enumerate(stages):
                    i = tick - si
                    if 0 <= i < n:
                        fn(i)

    load_head(0)
    for hi_, (kind, a, b) in enumerate(heads):
        if hi_ + 1 < len(heads):
            load_head(hi_ + 1)
        q, k, v, B_hs = HS[hi_ % 2]
        run_head(hi_, kind, a, b, q, k, v, B_hs)
    return yo.bufs


def phase_C(K, TOK, xT, xT_out, ydT, ysT, gT, w_ud, w_us, w_o, n2g, w1, w2, wring, st_eng="pool"):
    P = K.P
    nt = TOK // T
    u = K.uid
    K.uid += 1
    wpd = WPrep(K, "ud%d" % u, w_ud, 512, D, 4, 2048)
    wps = WPrep(K, "us%d" % u, w_us, 1024, D, 8, 1024)
    wpo = WPrep(K, "o%d" % u, w_o, D, D, 16, 512)
    wp1 = WPrep(K, "f1%d" % u, w1, D, DFF, 16, 512)
    wp2 = WPrep(K, "f2%d" % u, w2, DFF, D, 32, 256)
    ws = WStream(K, wring, 2)
    for t in range(nt):
        ws.add(wpd, 0, 0)
        ws.add(wps, 0, 0)
        ws.add(wps, 0, 1)
        for cb in range(4):
            ws.add(wpo, 0, cb)
        for half in range(2):
            for cb in range(8):
                ws.add(wp1, 0, half * 8 + cb)
            for cb in range(8):
                ws.add(wp2, half, cb)
    par = K.sb("parC%d" % u, [128, 16], F32); B_par = P.buf("parC")
    P.dma("sp", lambda e: e.dma_start(out=par[:], in_=n2g), writes=[B_par])
    x = K.sb("xC%d" % u, [128, 16, T], F32)
    B_x = [P.buf("xC%d" % i) for i in range(16)]
    yd = K.sb("ydC%d" % u, [128, 4, T], BF16); B_yd = P.buf("yd")
    ys = K.sb("ysC%d" % u, [128, 8, T], BF16); B_ys = P.buf("ys")
    gr = K.ring("gC%d" % u, [128, 2, T], BF16, 3)
    mixed = K.sb("mixC%d" % u, [128, 16, T], BF16)
    B_mix = [P.buf("mix%d" % i) for i in range(16)]
    h2 = K.sb("h2C%d" % u, [128, 16, T], BF16)
    B_h2 = [P.buf("h2%d" % i) for i in range(16)]
    fT = K.sb("fC%d" % u, [128, 32, T], BF16)
    B_f = [P.buf("f%d" % i) for i in range(32)]
    acc = K.ring("accC%d" % u, [128, T], F32, 4, psum=True)
    ss_ps = K.ps("ssC%d" % u, [128, T]); B_ss = P.buf("ssC")
    t1r = K.ring("t1C%d" % u, [128, T], F32, 2)
    t2r = K.ring("t2C%d" % u, [128, T], F32, 2)
    sqr = K.ring("sqC%d" % u, [128, T], BF16, 3)
    rstd = K.sb("rstdC%d" % u, [128, T], F32); B_rstd = P.buf("rstdC")
    lnt = K.sb("lntC%d" % u, [128, T], F32); B_lnt = P.buf("lntC")
    cbf = K.cbf
    B_c = K.B_c
    B_out = P.buf("xout")
    wi = 0
    for t in range(nt):
        t0 = t * T
        for kc in range(16):
            P.dma("sp", lambda e, kc=kc, t0=t0: e.dma_start(out=x[:, kc, :], in_=xT[kc * 128:(kc + 1) * 128, t0:t0 + T]), writes=[B_x[kc]])
        P.dma("sp", lambda e, t0=t0: e.dma_start(out=yd[:], in_=ydT[:, t0:t0 + T].rearrange("(kc p) t -> p kc t", p=128)), writes=[B_yd])
        P.dma("sp", lambda e, t0=t0: e.dma_start(out=ys[:], in_=ysT[:, t0:t0 + T].rearrange("(kc p) t -> p kc t", p=128)), writes=[B_ys])
        wd_t, B_wd = ws.get(wi, 2); wi += 1
        ws_t = []
        for i in range(2):
            ws_t.append(ws.get(wi, 1 - i)); wi += 1
        for c in range(16):
            gt, B_g = gr.next()
            P.dma("sp", lambda e, gt=gt, c=c, t0=t0: e.dma_start(out=gt[:, 0, :], in_=gT[c, :, t0:t0 + T]), writes=[B_g])
            P.dma("sp", lambda e, gt=gt, c=c, t0=t0: e.dma_start(out=gt[:, 1, :], in_=gT[16 + c, :, t0:t0 + T]), writes=[B_g])
            pa, B_pa = acc.next()
            for kc in range(4):
                P.op("pe", lambda e, pa=pa, kc=kc, c=c, wd_t=wd_t: e.matmul(pa[:], lhsT=wd_t[:, kc, c * 128:(c + 1) * 128], rhs=yd[:, kc, :], start=(kc == 0), stop=(kc == 3)),
                     reads=[B_wd, B_yd], writes=[B_pa])
            pb, B_pb = acc.next()
            wst, B_wst = ws_t[c // 8]
            cc = c % 8
            for kc in range(8):
                P.op("pe", lambda e, pb=pb, kc=kc, cc=cc, wst=wst: e.matmul(pb[:], lhsT=wst[:, kc, cc * 128:(cc + 1) * 128], rhs=ys[:, kc, :], start=(kc == 0), stop=(kc == 7)),
                     reads=[B_wst, B_ys], writes=[B_pb])
            t1, B_t1 = t1r.next()
            P.op("dve", lambda e, t1=t1, pa=pa, gt=gt: e.tensor_tensor(out=t1[:], in0=pa[:], in1=gt[:, 0, :], op=ALU.mult), reads=[B_pa, B_g], writes=[B_t1])
            t2, B_t2 = t2r.next()
            P.op("dve", lambda e, t2=t2, pb=pb, gt=gt: e.tensor_tensor(out=t2[:], in0=pb[:], in1=gt[:, 1, :], op=ALU.mult), reads=[B_pb, B_g], writes=[B_t2])
            P.op("pool", lambda e, t1=t1, t2=t2, c=c: e.tensor_tensor(out=mixed[:, c, :], in0=t1[:], in1=t2[:], op=ALU.add), reads=[B_t1, B_t2], writes=[B_mix[c]])
        for cb in range(4):
            wt, B_w = ws.get(wi); wi += 1
            for ci in range(4):
                c = cb * 4 + ci
                pa, B_pa = acc.next()
                for kc in range(16):
                    P.op("pe", lambda e, pa=pa, kc=kc, ci=ci, wt=wt: e.matmul(pa[:], lhsT=wt[:, kc, ci * 128:(ci + 1) * 128], rhs=mixed[:, kc, :], start=(kc == 0), stop=(kc == 15)),
                         reads=[B_w, B_mix[kc]], writes=[B_pa])
                P.op("dve", lambda e, pa=pa, c=c: e.tensor_tensor(out=x[:, c, :], in0=pa[:], in1=x[:, c, :], op=ALU.add), reads=[B_pa, B_x[c]], writes=[B_x[c]])
                sq, B_sq = sqr.next()
                P.op("act", lambda e, sq=sq, c=c: e.activation(out=sq[:], in_=x[:, c, :], func=AF.Square), reads=[B_x[c]], writes=[B_sq])
                P.op("pool", lambda e, c=c: e.tensor_scalar(out=h2[:, c, :], in0=x[:, c, :], scalar1=par[:, c:c + 1], scalar2=None, op0=ALU.mult),
                     reads=[B_x[c], B_par], writes=[B_h2[c]])
                P.op("pe", lambda e, sq=sq, c=c: e.matmul(ss_ps[:], lhsT=cbf[:, C_ONES:C_ONES + 128], rhs=sq[:], start=(c == 0), stop=(c == 15)),
                     reads=[B_sq, B_c], writes=[B_ss])
        rstd_from_ss(K, ss_ps[:], B_ss, rstd[:], B_rstd, lnt[:], B_lnt, 1.0 / D)
        for half in range(2):
            for cb in range(8):
                wt, B_w = ws.get(wi); wi += 1
                for ci in range(4):
                    fc = cb * 4 + ci
                    pa, B_pa = acc.next()
                    for kc in range(16):
                        P.op("pe", lambda e, pa=pa, kc=kc, ci=ci, wt=wt: e.matmul(pa[:], lhsT=wt[:, kc, ci * 128:(ci + 1) * 128], rhs=h2[:, kc, :], start=(kc == 0), stop=(kc == 15)),
                             reads=[B_w, B_h2[kc]], writes=[B_pa])
                    t1, B_t1 = t1r.next()
                    P.op("dve", lambda e, t1=t1, pa=pa: e.scalar_tensor_tensor(out=t1[:], in0=pa[:], scalar=0.0, in1=rstd[:], op0=ALU.max, op1=ALU.mult),
                         reads=[B_pa, B_rstd], writes=[B_t1])
                    P.op("act", lambda e, t1=t1, fc=fc: e.activation(out=fT[:, fc, :], in_=t1[:], func=AF.Square), reads=[B_t1], writes=[B_f[fc]])
            for cb in range(8):
                wt, B_w = ws.get(wi); wi += 1
                wt2 = wt
                for ci in range(2):
                    c = cb * 2 + ci
                    pa, B_pa = acc.next()
                    for kc in range(32):
                        P.op("pe", lambda e, pa=pa, kc=kc, ci=ci, wt2=wt2: e.matmul(pa[:], lhsT=wt2[:, kc, ci * 128:(ci + 1) * 128], rhs=fT[:, kc, :], start=(kc == 0), stop=(kc == 31)),
                             reads=[B_w, B_f[kc]], writes=[B_pa])
                    P.op("dve", lambda e, pa=pa, c=c: e.tensor_tensor(out=x[:, c, :], in0=pa[:], in1=x[:, c, :], op=ALU.add), reads=[B_pa, B_x[c]], writes=[B_x[c]])
                    if half == 1:
                        P.dma(st_eng, lambda e, c=c, t0=t0: e.dma_start(out=xT_out[c * 128:(c + 1) * 128, t0:t0 + T], in_=x[:, c, :]),
                              reads=[B_x[c]], writes=[B_out], sem_of=B_x[c])
    return [B_out]


def a_io(K, TOK, kind_out):
    outs = {
        "qdT": K.dram("qdT", [12, 128, TOK], BF16, kind_out),
        "kdT": K.dram("kdT", [12, 128, TOK], BF16, kind_out),
        "qsT": K.dram("qsT", [8, 128, TOK], BF16, kind_out),
        "ksT": K.dram("ksT", [8, 128, TOK], BF16, kind_out),
        "gT": K.dram("gT", [32, 128, TOK], BF16, kind_out),
        "vd": K.dram("vd", [TOK, 1536], BF16, kind_out),
        "vs": K.dram("vs", [TOK, 1024], BF16, kind_out),
    }
    return outs


def build_A(TOK):
    nc = bass.Bass("TRN2", target_bir_lowering=False)
    with ExitStack() as es:
        K = KB(nc, es)
        xT = K.dram("xT", [D, TOK], F32, "ExternalInput")
        n1g = K.dram("n1g", [128, 16], F32, "ExternalInput")
        w_in = K.dram("w_in", [D, NIN], F32, "ExternalInput")
        qg = K.dram("qg", [128, 3], F32, "ExternalInput")
        kg = K.dram("kg", [128, 3], F32, "ExternalInput")
        gb = K.dram("gb", [128, 32], F32, "ExternalInput")
        cosT = K.dram("cosT", [128, TOK], F32, "ExternalInput")
        sinT = K.dram("sinT", [128, TOK], F32, "ExternalInput")
        cbf = K.dram("cbf", [128, NCBF], BF16, "ExternalInput")
        cf32 = K.dram("cf32", [128, NCF32], F32, "ExternalInput")
        outs = a_io(K, TOK, "ExternalOutput")
        K.load_consts(cbf, cf32)
        wring = K.ring("wring", [128, 8192], BF16, 3)
        obufs = phase_A(K, TOK, xT, n1g, w_in, qg, kg, gb, cosT, sinT, outs, wring)
        K.P.finish("pool", obufs)
        K.P.run(nc)
    return nc


def build_B(S):
    nc = bass.Bass("TRN2", target_bir_lowering=False)
    with ExitStack() as es:
        K = KB(nc, es)
        ins = {
            "qdT": K.dram("qdT", [6, 128, S], BF16, "ExternalInput"),
            "kdT": K.dram("kdT", [6, 128, S], BF16, "ExternalInput"),
            "vd": K.dram("vd", [S, 768], BF16, "ExternalInput"),
            "qsT": K.dram("qsT", [4, 128, S], BF16, "ExternalInput"),
            "ksT": K.dram("ksT", [4, 128, S], BF16, "ExternalInput"),
            "vs": K.dram("vs", [S, 512], BF16, "ExternalInput"),
        }
        cbf = K.dram("cbf", [128, NCBF], BF16, "ExternalInput")
        cf32 = K.dram("cf32", [128, NCF32], F32, "ExternalInput")
        ydT = K.dram("ydT", [256, S], BF16, "ExternalOutput")
        ysT = K.dram("ysT", [512, S], BF16, "ExternalOutput")
        K.load_consts(cbf, cf32)
        obufs = phase_B(K, S, ins, ydT, ysT)
        K.P.finish("sp", obufs)
        K.P.run(nc)
    return nc


def build_C(TOK, with_A):
    nc = bass.Bass("TRN2", target_bir_lowering=False)
    with ExitStack() as es:
        K = KB(nc, es)
        xT = K.dram("xT", [D, TOK], F32, "ExternalInput")
        ydT = K.dram("ydT", [512, TOK], BF16, "ExternalInput")
        ysT = K.dram("ysT", [1024, TOK], BF16, "ExternalInput")
        gT = K.dram("gT_in", [32, 128, TOK], BF16, "ExternalInput")
        w_ud = K.dram("w_ud", [512, D], F32, "ExternalInput")
        w_us = K.dram("w_us", [1024, D], F32, "ExternalInput")
        w_o = K.dram("w_o", [D, D], F32, "ExternalInput")
        n2g = K.dram("n2g", [128, 16], F32, "ExternalInput")
        w1 = K.dram("w1", [D, DFF], F32, "ExternalInput")
        w2 = K.dram("w2", [DFF, D], F32, "ExternalInput")
        cbf = K.dram("cbf", [128, NCBF], BF16, "ExternalInput")
        cf32 = K.dram("cf32", [128, NCF32], F32, "ExternalInput")
        xo = K.dram("xT_out", [D, TOK], F32, "ExternalOutput")
        K.load_consts(cbf, cf32)
        wring = K.ring("wring", [128, 8192], BF16, 3)
        ob = phase_C(K, TOK, xT, xo, ydT, ysT, gT, w_ud, w_us, w_o, n2g, w1, w2, wring)
        if with_A:
            n1g = K.dram("n1g", [128, 16], F32, "ExternalInput")
            w_in = K.dram("w_in", [D, NIN], F32, "ExternalInput")
            qg = K.dram("qg", [128, 3], F32, "ExternalInput")
            kg = K.dram("kg", [128, 3], F32, "ExternalInput")
            gb = K.dram("gb", [128, 32], F32, "ExternalInput")
            cosT = K.dram("cosT", [128, TOK], F32, "ExternalInput")
            sinT = K.dram("sinT", [128, TOK], F32, "ExternalInput")
            outs = a_io(K, TOK, "ExternalOutput")
            xo_buf = ob[0]
            ob2 = phase_A_after(K, TOK, xo, xo_buf, n1g, w_in, qg, kg, gb, cosT, sinT, outs, wring)
            ob = ob + ob2
        K.P.finish("pool", ob)
        K.P.run(nc)
    return nc


def phase_A_after(K, TOK, xo, xo_buf, n1g, w_in, qg, kg, gb, cosT, sinT, outs, wring):
    K.P.finish("sp", [xo_buf])
    return phase_A(K, TOK, xo, n1g, w_in, qg, kg, gb, cosT, sinT, outs, wring)


_CACHE = {}
DBG = {}


def _prog(key, fn):
    if key not in _CACHE:
        _CACHE[key] = fn()
    return _CACHE[key]


def fm(v):
    return np.ascontiguousarray(v.reshape(-1, 128).T)


def run_model(x, norm1_g, w_in, q_norm_g, k_norm_g, w_up_dil, w_up_sb, gate_b, w_out, norm2_g, w_ff1, w_ff2):
    Bn, S, _ = x.shape
    TOK = S // 2
    cbf, cf32 = make_consts()
    cosT, sinT = rope_tables(S)
    cores = list(range(NCORES))
    xT = [np.ascontiguousarray(x[c // 2, (c % 2) * TOK:(c % 2 + 1) * TOK, :].T) for c in cores]

    def a_inputs(l):
        return {
            "n1g": fm(norm1_g[l]), "w_in": np.ascontiguousarray(w_in[l]),
            "qg": np.ascontiguousarray(q_norm_g[l].T), "kg": np.ascontiguousarray(k_norm_g[l].T),
            "gb": fm(gate_b[l].reshape(-1)),
        }

    def c_inputs(l):
        return {
            "w_ud": np.ascontiguousarray(w_up_dil[l]), "w_us": np.ascontiguousarray(w_up_sb[l]),
            "w_o": np.ascontiguousarray(w_out[l]), "n2g": fm(norm2_g[l]),
            "w1": np.ascontiguousarray(w_ff1[l]), "w2": np.ascontiguousarray(w_ff2[l]),
        }

    def rope_in(c):
        hf = c % 2
        return {"cosT": np.ascontiguousarray(cosT[:, hf * TOK:(hf + 1) * TOK]),
                "sinT": np.ascontiguousarray(sinT[:, hf * TOK:(hf + 1) * TOK])}

    def to_B(resA):
        maps = []
        for c in cores:
            b, sh = c // 2, c % 2
            r0, r1 = resA[2 * b], resA[2 * b + 1]
            cat = lambda k, ax: np.concatenate([r0[k], r1[k]], axis=ax)
            qd = cat("qdT", 2); kd = cat("kdT", 2); qs = cat("qsT", 2); ks = cat("ksT", 2)
            vd = cat("vd", 0); vs = cat("vs", 0)
            hd = [g * 4 + 2 * sh + j for g in range(NG) for j in range(2)]
            maps.append({
                "qdT": np.ascontiguousarray(qd[hd]), "kdT": np.ascontiguousarray(kd[hd]),
                "vd": np.ascontiguousarray(np.concatenate([vd[:, h * 128:(h + 1) * 128] for h in hd], axis=1)),
                "qsT": np.ascontiguousarray(qs[4 * sh:4 * sh + 4]), "ksT": np.ascontiguousarray(ks[4 * sh:4 * sh + 4]),
                "vs": np.ascontiguousarray(vs[:, 512 * sh:512 * sh + 512]),
                "cbf": cbf, "cf32": cf32,
            })
        return maps

    def to_C(resB, resA, xT_cur, l, with_A):
        maps = []
        for c in cores:
            b, hf = c // 2, c % 2
            r0, r1 = resB[2 * b], resB[2 * b + 1]
            sl = slice(hf * TOK, (hf + 1) * TOK)
            yd = np.concatenate([r0["ydT"][:, sl], r1["ydT"][:, sl]], axis=0)
            ys = np.concatenate([r0["ysT"][:, sl], r1["ysT"][:, sl]], axis=0)
            m = {"xT": xT_cur[c], "ydT": np.ascontiguousarray(yd), "ysT": np.ascontiguousarray(ys),
                 "gT_in": resA[c]["gT"], "cbf": cbf, "cf32": cf32}
            m.update(c_inputs(l))
            if with_A:
                m.update(a_inputs(l + 1))
                m.update(rope_in(c))
            maps.append(m)
        return maps

    ncA = _prog(("A", TOK), lambda: build_A(TOK))
    ncB = _prog(("B", S), lambda: build_B(S))
    mapsA = []
    for c in cores:
        m = {"xT": xT[c], "cbf": cbf, "cf32": cf32}
        m.update(a_inputs(0))
        m.update(rope_in(c))
        mapsA.append(m)
    resA = run_bass_kernel_spmd(ncA, mapsA, core_ids=cores).results
    xT_cur = xT
    DBG["A0"] = resA
    ncC = _prog(("C", TOK), lambda: build_C(TOK, False))
    for l in range(DEPTH):
        if l > 0:
            mapsA = []
            for c in cores:
                m = {"xT": xT_cur[c], "cbf": cbf, "cf32": cf32}
                m.update(a_inputs(l))
                m.update(rope_in(c))
                mapsA.append(m)
            resA = run_bass_kernel_spmd(ncA, mapsA, core_ids=cores).results
        resB = run_bass_kernel_spmd(ncB, to_B(resA), core_ids=cores).results
        resC = run_bass_kernel_spmd(ncC, to_C(resB, resA, xT_cur, l, False), core_ids=cores).results
        xT_cur = [r["xT_out"] for r in resC]
        DBG["B%d" % l] = resB
        DBG["C%d" % l] = xT_cur
    out = np.empty((Bn, S, D), np.float32)
    for c in cores:
        out[c // 2, (c % 2) * TOK:(c % 2 + 1) * TOK, :] = xT_cur[c].T
    return out


def kernel(x, norm1_g, w_in, q_norm_g, k_norm_g, w_up_dil, w_up_sb, gate_b, w_out, norm2_g, w_ff1, w_ff2):
    a = [np.asarray(v, dtype=np.float32) for v in
         (x, norm1_g, w_in, q_norm_g, k_norm_g, w_up_dil, w_up_sb, gate_b, w_out, norm2_g, w_ff1, w_ff2)]
    return run_model(*a)
```

```python
import math
from contextlib import ExitStack
import numpy as np
import ml_dtypes
import concourse.bass as bass
import concourse.mybir as mybir
from concourse.bass_utils import run_bass_kernel_spmd

F32 = mybir.dt.float32
BF16 = mybir.dt.bfloat16
AF = mybir.ActivationFunctionType
ALU = mybir.AluOpType

D = 2048
DEPTH = 2
HD = 128
NG = 3
DIL = (1, 4, 16)
DILW = 1536
SBW = 1024
NIN = 11776
DFF = 8192
EPS = 1e-6
T = 512
NCORES = 8
NEG = -30000.0
ISQ = 1.0 / math.sqrt(128.0)

ENGS = ("pe", "act", "dve", "pool", "sp")


class Buf:
    __slots__ = ("name", "w", "r", "lsem", "ssem")

    def __init__(self, name):
        self.name = name
        self.w = {}
        self.r = {}
        self.lsem = None
        self.ssem = None


class Prog:
    def __init__(self):
        self.streams = {e: [] for e in ENGS}
        self.phase = 0
        self.count = {e: 0 for e in ENGS}
        self.waited = {e: {} for e in ENGS}
        self.dma_val = {}
        self.n_dma_sems = 0
        self.free_sems = []
        self.used_sems = []

    def buf(self, name=""):
        return Buf(name)

    def ek(self, eng):
        return ("e", self.phase, eng)

    def _new_dma_sem(self):
        if self.free_sems:
            k = self.free_sems.pop()
        else:
            k = ("d", self.n_dma_sems)
            self.n_dma_sems += 1
            self.dma_val[k] = 0
        self.used_sems.append(k)
        return k

    def barrier(self):
        need = {}
        for e in ENGS:
            if self.count[e] > 0:
                need[self.ek(e)] = self.count[e]
        for k, v in self.dma_val.items():
            if v > 0:
                need[k] = v
        for e in ENGS:
            wd = self.waited[e]
            waits = []
            for k, v in need.items():
                if k == self.ek(e):
                    continue
                if wd.get(k, 0) < v:
                    wd[k] = v
                    waits.append((k, v))
            self.streams[e].append((waits, None, None))
        self.phase += 1
        self.count = {e: 0 for e in ENGS}
        self.free_sems.extend(self.used_sems)
        self.used_sems = []

    def _collect(self, eng, reads, writes):
        need = {}
        own = self.ek(eng)
        for b in reads:
            for k, v in b.w.items():
                if need.get(k, 0) < v:
                    need[k] = v
        for b in writes:
            for k, v in b.w.items():
                if need.get(k, 0) < v:
                    need[k] = v
            for k, v in b.r.items():
                if k == own:
                    continue
                if need.get(k, 0) < v:
                    need[k] = v
        wd = self.waited[eng]
        out = []
        for k, v in need.items():
            if k[0] == "e" and k[1] != self.phase:
                continue
            if k == own and eng in ("pe", "sp"):
                continue
            if wd.get(k, 0) < v:
                wd[k] = v
                out.append((k, v))
        return out

    def op(self, eng, fn, reads=(), writes=()):
        waits = self._collect(eng, reads, writes)
        self.count[eng] += 1
        v = self.count[eng]
        k = self.ek(eng)
        self.streams[eng].append((waits, fn, (k, 1)))
        for b in reads:
            b.r[k] = v
        for b in writes:
            b.w[k] = v

    def _sem_for(self, b, attr):
        cur = getattr(b, attr)
        if cur is None or cur[0] != self.phase:
            cur = (self.phase, self._new_dma_sem())
            setattr(b, attr, cur)
        return cur[1]

    def dma(self, eng, fn, reads=(), writes=(), sem_of=None, inc=16, after=()):
        waits = self._collect(eng, reads, list(writes) + list(after))
        if sem_of is not None:
            k = self._sem_for(sem_of, "lsem")
        elif writes:
            k = self._sem_for(writes[0], "lsem")
        else:
            k = self._sem_for(reads[0], "ssem")
        self.dma_val[k] += inc
        v = self.dma_val[k]
        self.streams[eng].append((waits, fn, (k, inc)))
        for b in reads:
            b.r[k] = v
        for b in writes:
            b.w[k] = v

    def run(self, nc):
        with ExitStack() as es:
            sems = {}
            for p in range(self.phase + 1):
                for e in ENGS:
                    sems[("e", p, e)] = es.enter_context(nc.semaphore("s%d_%s" % (p, e)))
            for i in range(self.n_dma_sems):
                sems[("d", i)] = es.enter_context(nc.semaphore("sd%d" % i))
            block = es.enter_context(nc.Block())
            streams = self.streams

            def play(engobj, name):
                for waits, fn, inc in streams[name]:
                    for k, v in waits:
                        engobj.wait_ge(sems[k], v)
                    if fn is not None:
                        fn(engobj).then_inc(sems[inc[0]], inc[1])

            @block.tensor
            def _(e):
                play(e, "pe")

            @block.scalar
            def _(e):
                play(e, "act")

            @block.vector
            def _(e):
                play(e, "dve")

            @block.gpsimd
            def _(e):
                play(e, "pool")

            @block.sync
            def _(e):
                play(e, "sp")


class Ring:
    def __init__(self, K, name, shape, dt, n, psum=False):
        self.tiles = [(K.ps(name + str(i), shape) if psum else K.sb(name + str(i), shape, dt)) for i in range(n)]
        self.bufs = [K.P.buf(name + str(i)) for i in range(n)]
        self.i = 0

    def next(self):
        j = self.i % len(self.tiles)
        self.i += 1
        return self.tiles[j], self.bufs[j]


C_ONES = 0
C_UNEG = 128
C_NONES = 256
C_IDENT = 384
C_NEGW = 512
C_NEGD = 1408
NCBF = 1664
F_ONES = 0
F_PERM = 128
F_INV = 256
NCF32 = 257


def make_consts():
    cb = np.zeros((128, NCBF), np.float32)
    j = np.arange(128)[:, None]
    s = np.arange(128)[None, :]
    cb[:, C_ONES:C_ONES + 128] = 1.0
    cb[:, C_UNEG:C_UNEG + 128] = np.where(j >= s, -1.0, 0.0)
    cb[:, C_NONES:C_NONES + 128] = -1.0
    cb[:, C_IDENT:C_IDENT + 128] = np.eye(128)
    c = np.arange(896)[None, :]
    cb[:, C_NEGW:C_NEGW + 896] = np.where((c - 384) <= j, NEG, 0.0)
    cb[:, C_NEGD:C_NEGD + 128] = np.where(j >= s, 0.0, NEG)
    cb[:, C_NEGD + 128:C_NEGD + 256] = np.where(j <= s, 0.0, NEG)
    cf = np.zeros((128, NCF32), np.float32)
    cf[:, F_ONES:F_ONES + 128] = 1.0
    cf[:, F_PERM:F_PERM + 128] = (j == ((s + 64) % 128)).astype(np.float32)
    cf[:, F_INV] = 1.0 / 128.0
    return cb.astype(ml_dtypes.bfloat16), cf


def rope_tables(S):
    half = HD // 2
    inv_freq = (np.float32(10000.0) ** (-np.arange(half, dtype=np.float32) / np.float32(half))).astype(np.float32)
    ang = (np.arange(S, dtype=np.float32)[:, None] * inv_freq[None, :]).astype(np.float32)
    cos = np.cos(ang).astype(np.float32)
    sin = np.sin(ang).astype(np.float32)
    cosT = np.concatenate([cos, cos], axis=1).T
    sinT = np.concatenate([-sin, sin], axis=1).T
    return np.ascontiguousarray(cosT), np.ascontiguousarray(sinT)


ARENA_W = 192 * 256


class KB:
    def __init__(self, nc, es):
        self.nc = nc
        self.es = es
        self.P = Prog()
        self.uid = 0
        self.arena = es.enter_context(nc.sbuf_tensor("arena", [128, ARENA_W], F32))
        self.banks = [es.enter_context(nc.psum_tensor("bank%d" % i, [128, 512], F32)) for i in range(8)]
        self.base = 0
        self.off = 0
        self.bank_i = 0

    def new_phase(self):
        self.P.barrier()
        self.off = self.base
        self.bank_i = 0

    def sb(self, name, shape, dt):
        n = 1
        for d in shape[1:]:
            n *= d
        esz = 4 if dt == F32 else 2
        words = (n * esz + 3) // 4
        words = (words + 7) // 8 * 8
        assert self.off + words <= ARENA_W, ("arena overflow", name, self.off, words)
        ap = self.arena[:, self.off:self.off + words]
        self.off += words
        if dt != F32:
            ap = ap.bitcast(dt)
        ap = ap[:, 0:n]
        if len(shape) == 3:
            ap = ap.rearrange("p (a b) -> p a b", a=shape[1])
        return ap

    def ps(self, name, shape, dt=F32):
        assert self.bank_i < 8, "psum overflow"
        b = self.banks[self.bank_i]
        self.bank_i += 1
        return b[:, 0:shape[1]]

    def dynview(self, e, key, mk):
        if not hasattr(self, "_dv"):
            self._dv = {}
            self._hf = e.partition_id() % 2
        if key not in self._dv:
            self._dv[key] = mk(self._hf)
        return self._dv[key]

    def dram(self, name, shape, dt, kind="Internal"):
        return self.nc.dram_tensor(name, shape, dt, kind=kind).ap()

    def ring(self, name, shape, dt, n, psum=False):
        return Ring(self, name, shape, dt, n, psum)

    def load_consts(self, cbf_ap, cf32_ap):
        self.cbf = self.sb("cbf_sb", [128, NCBF], BF16)
        self.cf = self.sb("cf32_sb", [128, NCF32], F32)
        self.base = self.off
        self.B_c = self.P.buf("consts")
        self.P.dma("sp", lambda e: e.dma_start(out=self.cbf[:], in_=cbf_ap), writes=[self.B_c])
        self.P.dma("sp", lambda e: e.dma_start(out=self.cf[:], in_=cf32_ap), writes=[self.B_c])


class WPrep:
    def __init__(self, K, name, w_ap, Kdim, N, kcb, ncols, ngroups=1):
        self.kcb, self.ncols = kcb, ncols
        self.nkb = Kdim // (128 * kcb)
        self.ncb = N // ncols
        self.scr = K.dram("wscr_" + name, [self.nkb, self.ncb, 128, kcb * ncols], BF16)
        self.bufs = {}
        wv = w_ap.rearrange("(kc p) n -> p kc n", p=128)
        gb_ = [K.P.buf("w_%s_%d" % (name, i)) for i in range(ngroups)]
        per = -(-(self.nkb * self.ncb) // ngroups)
        for kb in range(self.nkb):
            for cb in range(self.ncb):
                b = gb_[(kb * self.ncb + cb) // per]
                self.bufs[(kb, cb)] = b
                dst = self.scr[kb, cb].rearrange("p (kc n) -> p kc n", kc=kcb)
                src = wv[:, kb * kcb:(kb + 1) * kcb, cb * ncols:(cb + 1) * ncols]
                K.P.dma("pool", lambda e, dst=dst, src=src: e.dma_start(out=dst, in_=src), writes=[b])


class WStream:
    def __init__(self, K, ring, depth):
        self.K, self.ring, self.depth = K, ring, depth
        self.plan = []
        self.loaded = []
        self.emitted = 0

    def add(self, wp, kb, cb):
        self.plan.append((wp, kb, cb))
        return len(self.plan) - 1

    def get(self, j, depth=None):
        hi = min(len(self.plan), j + (self.depth if depth is None else depth) + 1)
        while self.emitted < hi:
            wp, kb, cb = self.plan[self.emitted]
            tile, b = self.ring.next()
            src = wp.scr[kb, cb]
            self.K.P.dma("sp", lambda e, tile=tile, src=src: e.dma_start(out=tile[:], in_=src),
                         reads=[wp.bufs[(kb, cb)]], writes=[b])
            self.loaded.append((tile, b, wp))
            self.emitted += 1
        tile, b, wp = self.loaded[j]
        return tile[:].rearrange("p (kc n) -> p kc n", kc=wp.kcb), b


def rstd_from_ss(K, ss_ps, B_ss, out_tile, B_out, tmp, B_tmp, inv_n):
    K.P.op("act", lambda e: e.activation(out=tmp, in_=ss_ps, func=AF.Ln, bias=EPS, scale=inv_n),
           reads=[B_ss], writes=[B_tmp])
    K.P.op("act", lambda e: e.activation(out=out_tile, in_=tmp, func=AF.Exp, scale=-0.5),
           reads=[B_tmp], writes=[B_out])


def phase_A(K, TOK, xT, n1g, wp, qg, kg, gb, cosT, sinT, aT, aV, gT, tile_done, st_eng="pool"):
    P = K.P
    nt = TOK // T
    wring = K.ring("wringA", [128, 8192], BF16, 3)
    ws = WStream(K, wring, 2)
    order = [0, 1, 2, 3, 4, 5, 9, 10, 11, 12, 15, 16, 17, 18, 19, 20, 21, 22, 6, 7, 8, 13, 14]
    for t in range(nt):
        for cb in order:
            ws.add(wp, 0, cb)
    par = K.sb("parA%d" % K.uid, [128, 16 + 3 + 3 + 32], F32)
    B_par = P.buf("parA")
    P.dma("sp", lambda e: e.dma_start(out=par[:, 0:16], in_=n1g), writes=[B_par])
    P.dma("sp", lambda e: e.dma_start(out=par[:, 16:19], in_=qg), writes=[B_par])
    P.dma("sp", lambda e: e.dma_start(out=par[:, 19:22], in_=kg), writes=[B_par])
    P.dma("sp", lambda e: e.dma_start(out=par[:, 22:54], in_=gb), writes=[B_par])

    hT = K.sb("hT%d" % K.uid, [128, 16, T], BF16)
    B_h = [P.buf("h%d" % i) for i in range(16)]
    xr = K.ring("xr%d" % K.uid, [128, T], F32, 3)
    sqr = K.ring("sqr%d" % K.uid, [128, T], BF16, 3)
    rstd = K.sb("rstd%d" % K.uid, [128, T], F32); B_rstd = P.buf("rstd")
    lnt = K.sb("lnt%d" % K.uid, [128, T], F32); B_lnt = P.buf("lnt")
    rstdT = K.sb("rstdT%d" % K.uid, [128, 4], F32); B_rstdT = P.buf("rstdT")
    cs = K.sb("cs%d" % K.uid, [128, 2, T], F32); B_cs = P.buf("cs")
    acc = K.ring("accA%d" % K.uid, [128, T], F32, 3, psum=True)
    ss_ps = K.ps("ssA%d" % K.uid, [128, T]); B_ss = P.buf("ssA")
    s2_ps = K.ps("s2A%d" % K.uid, [128, T]); B_s2 = P.buf("s2A")
    pm_ps = K.ps("pmA%d" % K.uid, [128, T]); B_pm = P.buf("pmA")
    rt_ps = K.ps("rtA%d" % K.uid, [128, 4]); B_rt = P.buf("rtA")
    tr = K.ring("tA%d" % K.uid, [128, T], F32, 2)
    sq2 = K.ring("sq2A%d" % K.uid, [128, T], F32, 4)
    l2 = K.ring("l2A%d" % K.uid, [128, T], F32, 2)
    tn = K.ring("tnA%d" % K.uid, [128, T], F32, 2)
    ra = K.ring("raA%d" % K.uid, [128, T], F32, 2)
    rb = K.ring("rbA%d" % K.uid, [128, T], F32, 2)
    ob = K.ring("obA%d" % K.uid, [128, T], BF16, 4)
    K.uid += 1
    cbf, cf = K.cbf, K.cf
    B_c = K.B_c
    wi = 0
    pipe = []

    def advance(new=None):
        for ent in list(pipe):
            ent[1][ent[0]]()
            ent[0] += 1
        if new is not None:
            new[0]()
            pipe.append([1, new])
        pipe[:] = [en for en in pipe if en[0] < 3]

    for t in range(nt):
        t0 = t * T
        P.dma("sp", lambda e, t0=t0: e.dma_start(out=cs[:, 0, :], in_=cosT[:, t0:t0 + T]), writes=[B_cs])
        P.dma("sp", lambda e, t0=t0: e.dma_start(out=cs[:, 1, :], in_=sinT[:, t0:t0 + T]), writes=[B_cs])
        for kc in range(16):
            xt, B_x = xr.next()
            P.dma("sp", lambda e, xt=xt, kc=kc, t0=t0: e.dma_start(out=xt[:], in_=xT[kc * 128:(kc + 1) * 128, t0:t0 + T]),
                  writes=[B_x])
            sq, B_sq = sqr.next()
            P.op("act", lambda e, sq=sq, xt=xt: e.activation(out=sq[:], in_=xt[:], func=AF.Square), reads=[B_x], writes=[B_sq])
            P.op("dve", lambda e, xt=xt, kc=kc: e.tensor_scalar(out=hT[:, kc, :], in0=xt[:], scalar1=par[:, kc:kc + 1], scalar2=None,
                                                               op0=ALU.mult), reads=[B_x, B_par], writes=[B_h[kc]])
            P.op("pe", lambda e, sq=sq, kc=kc: e.matmul(ss_ps[:], lhsT=cbf[:, C_ONES:C_ONES + 128], rhs=sq[:],
                                                        start=(kc == 0), stop=(kc == 15)), reads=[B_sq, B_c], writes=[B_ss])
        rstd_from_ss(K, ss_ps[:], B_ss, rstd[:], B_rstd, lnt[:], B_lnt, 1.0 / D)
        for j in range(4):
            P.op("pe", lambda e, j=j: e.matmul(rt_ps[:, j:j + 1], lhsT=rstd[:, j * 128:(j + 1) * 128], rhs=cf[:, F_INV:F_INV + 1],
                                               start=True, stop=True), reads=[B_rstd, B_c], writes=[B_rt])
        P.op("dve", lambda e: e.tensor_copy(out=rstdT[:], in_=rt_ps[:]), reads=[B_rt], writes=[B_rstdT])

        for cb in order:
            wt, B_w = ws.get(wi)
            wi += 1
            if cb in (6, 7, 8, 13, 14):
                for j in range(4):
                    ps, B_ps = acc.next()
                    for kc in range(16):
                        P.op("pe", lambda e, ps=ps, wt=wt, kc=kc, j=j: e.matmul(ps[:], lhsT=hT[:, kc, j * 128:(j + 1) * 128], rhs=wt[:, kc, :],
                                                                                 start=(kc == 0), stop=(kc == 15)),
                             reads=[B_h[kc], B_w], writes=[B_ps])
                    o, B_o = ob.next()
                    P.op("act", lambda e, o=o, ps=ps, j=j: e.activation(out=o[:], in_=ps[:], func=AF.Identity, scale=rstdT[:, j:j + 1]),
                         reads=[B_ps, B_rstdT], writes=[B_o])
                    r0 = t0 + j * 128
                    if cb < 9:
                        g_ = cb - 6
                        for shh in range(2):
                            dc = shh * 1280 + g_ * 256
                            P.dma(st_eng, lambda e, o=o, r0=r0, dc=dc, shh=shh: e.dma_start(out=aV[r0:r0 + 128, dc:dc + 256], in_=o[:, shh * 256:(shh + 1) * 256]),
                                  reads=[B_o])
                    else:
                        dc = (cb - 13) * 1280 + 768
                        P.dma(st_eng, lambda e, o=o, r0=r0, dc=dc: e.dma_start(out=aV[r0:r0 + 128, dc:dc + 512], in_=o[:]), reads=[B_o])
                continue
            for ci in range(4):
                ch = cb * 4 + ci
                ps, B_ps = acc.next()
                for kc in range(16):
                    P.op("pe", lambda e, ps=ps, wt=wt, kc=kc, ci=ci: e.matmul(ps[:], lhsT=wt[:, kc, ci * 128:(ci + 1) * 128], rhs=hT[:, kc, :],
                                                                               start=(kc == 0), stop=(kc == 15)),
                         reads=[B_h[kc], B_w], writes=[B_ps])
                o, B_o = ob.next()
                if ch < 24:
                    isq = ch < 12
                    hidx = ch if isq else ch - 12
                    g = hidx // 4
                    gcol = (16 if isq else 19) + g
                    c0 = ISQ if isq else 1.0
                    slot_ = hidx % 4
                    rb_ = (slot_ // 2) * 20 + (0 if isq else 6) + g * 2 + slot_ % 2
                    ob.i -= 1

                    def mk(ps=ps, B_ps=B_ps, gcol=gcol, c0=c0, rb_=rb_, t0=t0):
                        d = {}

                        def E1():
                            tt, B_t = tr.next()
                            d["t"] = (tt, B_t)
                            P.op("dve", lambda e: e.tensor_tensor(out=tt[:], in0=ps[:], in1=rstd[:], op=ALU.mult),
                                 reads=[B_ps, B_rstd], writes=[B_t])
                            s2, B_q2 = sq2.next()
                            d["s2"] = (s2, B_q2)
                            P.op("act", lambda e: e.activation(out=s2[:], in_=tt[:], func=AF.Square), reads=[B_t], writes=[B_q2])

                        def E2():
                            tt, B_t = d["t"]
                            s2, B_q2 = d["s2"]
                            P.op("pe", lambda e: e.matmul(s2_ps[:], lhsT=cf[:, F_ONES:F_ONES + 128], rhs=s2[:], start=True, stop=True),
                                 reads=[B_q2, B_c], writes=[B_s2])
                            ll, B_l = l2.next()
                            r2, B_r2 = sq2.next()
                            rstd_from_ss(K, s2_ps[:], B_s2, r2[:], B_r2, ll[:], B_l, 1.0 / HD)
                            nn, B_n = tn.next()
                            d["n"] = (nn, B_n)
                            P.op("dve", lambda e: e.scalar_tensor_tensor(out=nn[:], in0=tt[:], scalar=par[:, gcol:gcol + 1],
                                                                         in1=r2[:], op0=ALU.mult, op1=ALU.mult),
                                 reads=[B_t, B_r2, B_par], writes=[B_n])

                        def E3():
                            nn, B_n = d["n"]
                            P.op("pe", lambda e: e.matmul(pm_ps[:], lhsT=cf[:, F_PERM:F_PERM + 128], rhs=nn[:], start=True, stop=True),
                                 reads=[B_n, B_c], writes=[B_pm])
                            aa, B_a = ra.next()
                            P.op("dve", lambda e: e.scalar_tensor_tensor(out=aa[:], in0=nn[:], scalar=c0, in1=cs[:, 0, :],
                                                                         op0=ALU.mult, op1=ALU.mult),
                                 reads=[B_n, B_cs], writes=[B_a])
                            bb, B_b = rb.next()
                            P.op("dve", lambda e: e.scalar_tensor_tensor(out=bb[:], in0=pm_ps[:], scalar=c0, in1=cs[:, 1, :],
                                                                         op0=ALU.mult, op1=ALU.mult),
                                 reads=[B_pm, B_cs], writes=[B_b])
                            o, B_o = ob.next()
                            P.op("dve", lambda e: e.tensor_tensor(out=o[:], in0=aa[:], in1=bb[:], op=ALU.add),
                                 reads=[B_a, B_b], writes=[B_o])
                            P.dma(st_eng, lambda e: e.dma_start(out=aT[t0 // T, rb_ * 128:(rb_ + 1) * 128, :], in_=o[:]), reads=[B_o])

                        return [E1, E2, E3]

                    advance(mk())
                    continue
                advance()
                if ch < 60:
                    isq = ch < 44
                    hidx = ch - 36 if isq else ch - 44
                    rb_ = (hidx // 4) * 20 + (12 if isq else 16) + hidx % 4
                    c0 = ISQ if isq else 1.0
                    P.op("dve", lambda e, o=o, ps=ps, c0=c0: e.scalar_tensor_tensor(out=o[:], in0=ps[:], scalar=c0, in1=rstd[:],
                                                                                    op0=ALU.mult, op1=ALU.mult),
                         reads=[B_ps, B_rstd], writes=[B_o])
                    P.dma(st_eng, lambda e, o=o, rb_=rb_, t0=t0: e.dma_start(out=aT[t0 // T, rb_ * 128:(rb_ + 1) * 128, :], in_=o[:]), reads=[B_o])
                else:
                    gi = ch - 60
                    tt, B_t = tr.next()
                    P.op("dve", lambda e, tt=tt, ps=ps: e.tensor_tensor(out=tt[:], in0=ps[:], in1=rstd[:], op=ALU.mult),
                         reads=[B_ps, B_rstd], writes=[B_t])
                    P.op("act", lambda e, o=o, tt=tt, gi=gi: e.activation(out=o[:], in_=tt[:], func=AF.Sigmoid, bias=par[:, 22 + gi:23 + gi], scale=1.0),
                         reads=[B_t, B_par], writes=[B_o])
                    P.dma(st_eng, lambda e, o=o, gi=gi, t0=t0: e.dma_start(out=gT[gi, :, t0:t0 + T], in_=o[:]), reads=[B_o])
        tile_done(t, ob.bufs)


def phase_B(K, S, aTg, aVg, bY, vloc, tloc, head_done, st_eng="sp"):
    P = K.P
    NB = S // 128
    cbf = K.cbf
    B_c = K.B_c
    HS = []
    for i in range(2):
        q = K.sb("hsq%d" % i, [128, S], BF16)
        k = K.sb("hsk%d" % i, [128, S], BF16)
        v = K.sb("hsv%d" % i, [128, NB, 128], BF16)
        HS.append((q, k, v, P.buf("hs%d" % i)))
    accO = K.sb("accO", [128, S], F32); B_accO = P.buf("accO")
    accD = K.sb("accD", [128, S], F32); B_accD = P.buf("accD")
    heads = []
    for sl in range(2):
        for g in range(NG):
            heads.append(("d", g, sl))
    for h in range(4):
        heads.append(("s", h, 0))

    TOKh = S // 2
    B_vloc = P.buf("vloc")
    B_tloc = P.buf("tloc")
    for rk in range(2):
        for jj in range(2):
            P.dma("sp", lambda e, rk=rk, jj=jj: e.dma_start(
                out=tloc[rk, jj * 1280:(jj + 1) * 1280, :].rearrange("x (t c) -> t x c", c=T),
                in_=K.dynview(e, "aTg", lambda hf: aTg[:, bass.DynSlice(hf * 2, 2)])[:, jj, rk]), writes=[B_tloc])
    for rk in range(2):
        P.dma("sp", lambda e, rk=rk: e.dma_start(out=vloc[rk * TOKh:(rk + 1) * TOKh, :].rearrange("(c x) d -> c x d", x=256),
                                                 in_=K.dynview(e, "aVg", lambda hf: aVg[:, :, :, bass.DynSlice(hf * 1280, 1280)])[:, rk]), writes=[B_vloc])

    def load_head(i):
        kind, a, b = heads[i]
        q, k, v, B = HS[i % 2]
        if kind == "d":
            hi = a * 2 + b
            r = DIL[a]
            nb = NB // r
            qb, kb_, vc = hi, 6 + hi, hi * 128
        else:
            qb, kb_, vc = 12 + a, 16 + a, 768 + a * 128
        for rk in range(2):
            P.dma("sp", lambda e, rk=rk: e.dma_start(out=q[:, rk * TOKh:(rk + 1) * TOKh],
                                                     in_=tloc[rk, qb * 128:(qb + 1) * 128, :]), reads=[B_tloc], writes=[B])
            P.dma("sp", lambda e, rk=rk: e.dma_start(out=k[:, rk * TOKh:(rk + 1) * TOKh],
                                                     in_=tloc[rk, kb_ * 128:(kb_ + 1) * 128, :]), reads=[B_tloc], writes=[B])
        if kind == "d":
            for c in range(r):
                P.dma("sp", lambda e, c=c: e.dma_start(
                    out=v[:, c * nb:(c + 1) * nb, :],
                    in_=vloc[:, vc:vc + 128].rearrange("(n i c) d -> i c n d", i=128, c=r)[:, c]), reads=[B_vloc], writes=[B])
        else:
            P.dma("sp", lambda e: e.dma_start(
                out=v[:], in_=vloc[:, vc:vc + 128].rearrange("(n i) d -> i n d", i=128)), reads=[B_vloc], writes=[B])

    p1 = K.ring("p1", [128, 512], F32, 2, psum=True)
    p2 = K.ring("p2", [128, 512], F32, 2, psum=True)
    po = K.ring("po", [128, 512], F32, 2, psum=True)
    pd = K.ring("pd", [128, 512], F32, 2, psum=True)
    er = K.ring("er", [128, 512], F32, 4)
    ecr = K.ring("ecr", [128, 512], F32, 2)
    spr = K.ring("spr", [128, 512], BF16, 3)
    lsum = K.sb("lsum", [128, 512], F32); B_lsum = P.buf("lsum")
    lbr = K.ring("lbr", [128, 512], BF16, 2)
    ar = K.ring("ar", [128, 512], BF16, 3)
    yo = K.ring("yo", [128, 512], BF16, 2)
    rcp = er

    def run_head(hi_, kind, a, b, q, k, v, B_hs):
        if kind == "d":
            g, sl = a, b
            r = DIL[g]
            nb = NB // r
            tiles = [(c, n) for c in range(r) for n in range(nb)]
            nt_ = len(tiles)
            dst = [dict() for _ in range(nt_)]

            def sl_(c, nn):
                st0 = c + r * 128 * nn
                return slice(st0, st0 + 127 * r + 1, r)

            def D0(t):
                c, n = tiles[t]
                qs = sl_(c, n)
                ps, B_ps = p1.next()
                dst[t]["ps"] = (ps, B_ps)
                if n > 0:
                    ks = sl_(c, n - 1)
                    P.op("pe", lambda e: e.matmul(ps[:, 0:128], lhsT=k[:, ks], rhs=q[:, qs], start=True, stop=False), reads=[B_hs], writes=[B_ps])
                    P.op("pe", lambda e: e.matmul(ps[:, 0:128], lhsT=cbf[:, C_IDENT:C_IDENT + 128], rhs=cbf[:, C_NEGD:C_NEGD + 128],
                                                  start=False, stop=True), reads=[B_c], writes=[B_ps])
                P.op("pe", lambda e: e.matmul(ps[:, 128:256], lhsT=k[:, qs], rhs=q[:, qs], start=True, stop=False), reads=[B_hs], writes=[B_ps])
                P.op("pe", lambda e: e.matmul(ps[:, 128:256], lhsT=cbf[:, C_IDENT:C_IDENT + 128], rhs=cbf[:, C_NEGD + 128:C_NEGD + 256],
                                              start=False, stop=True), reads=[B_c], writes=[B_ps])

            def D1(t):
                c, n = tiles[t]
                ps, B_ps = dst[t]["ps"]
                lo = 0 if n > 0 else 128
                pt, B_pt = ar.next()
                dst[t]["pt"] = (pt, B_pt)
                P.op("act", lambda e: e.activation(out=pt[:, lo:256], in_=ps[:, lo:256], func=AF.Exp), reads=[B_ps], writes=[B_pt])

            def D2(t):
                c, n = tiles[t]
                pt, B_pt = dst[t]["pt"]
                o_ps, B_o = po.next()
                d_ps, B_d = pd.next()
                dst[t]["o"] = (o_ps, B_o, d_ps, B_d)
                ti = c * nb + n
                if n > 0:
                    P.op("pe", lambda e: e.matmul(o_ps[:, 0:128], lhsT=v[:, ti - 1, :], rhs=pt[:, 0:128], start=True, stop=False),
                         reads=[B_hs, B_pt], writes=[B_o])
                P.op("pe", lambda e: e.matmul(o_ps[:, 0:128], lhsT=v[:, ti, :], rhs=pt[:, 128:256], start=(n == 0), stop=True),
                     reads=[B_hs, B_pt], writes=[B_o])
                if n > 0:
                    P.op("pe", lambda e: e.matmul(d_ps[:, 0:128], lhsT=cbf[:, C_ONES:C_ONES + 128], rhs=pt[:, 0:128], start=True, stop=False),
                         reads=[B_c, B_pt], writes=[B_d])
                P.op("pe", lambda e: e.matmul(d_ps[:, 0:128], lhsT=cbf[:, C_ONES:C_ONES + 128], rhs=pt[:, 128:256], start=(n == 0), stop=True),
                     reads=[B_c, B_pt], writes=[B_d])

            def D3(t):
                c, n = tiles[t]
                qs = sl_(c, n)
                o_ps, B_o, d_ps, B_d = dst[t]["o"]
                if g == 0:
                    P.op("dve", lambda e: e.tensor_copy(out=accO[:, qs], in_=o_ps[:, 0:128]), reads=[B_o], writes=[B_accO])
                    P.op("dve", lambda e: e.tensor_copy(out=accD[:, qs], in_=d_ps[:, 0:128]), reads=[B_d], writes=[B_accD])
                else:
                    P.op("dve", lambda e: e.tensor_tensor(out=accO[:, qs], in0=o_ps[:, 0:128], in1=accO[:, qs], op=ALU.add),
                         reads=[B_o, B_accO], writes=[B_accO])
                    P.op("dve", lambda e: e.tensor_tensor(out=accD[:, qs], in0=d_ps[:, 0:128], in1=accD[:, qs], op=ALU.add),
                         reads=[B_d, B_accD], writes=[B_accD])
                dst[t].clear()

            dstages = [D0, D1, D2, D3]
            for tick in range(nt_ + 3):
                for si, fn in enumerate(dstages):
                    t = tick - si
                    if 0 <= t < nt_:
                        fn(t)
            if g == NG - 1:
                for j in range(S // 512):
                    rc, B_rc = rcp.next()
                    P.op("dve", lambda e, rc=rc, j=j: e.reciprocal(out=rc[:], in_=accD[:, j * 512:(j + 1) * 512]), reads=[B_accD], writes=[B_rc])
                    y, B_y = yo.next()
                    P.op("dve", lambda e, y=y, rc=rc, j=j: e.tensor_tensor(out=y[:], in0=accO[:, j * 512:(j + 1) * 512], in1=rc[:], op=ALU.mult),
                         reads=[B_accO, B_rc], writes=[B_y])
                    P.dma(st_eng, lambda e, y=y, j=j, sl=sl: e.dma_start(out=bY[sl * 128:(sl + 1) * 128, j * 512:(j + 1) * 512], in_=y[:]), reads=[B_y])
                head_done(sl, yo.bufs)
        else:
            h = a
            steps = []
            for qg_ in range(S // 512):
                for kb in range(4 * qg_ + 3, -1, -1):
                    steps.append((qg_, kb))
            n = len(steps)
            st = [dict() for _ in range(n)]
            cur_o = [None]

            def mask_mm(e, ps, o):
                return e.matmul(ps[:], lhsT=cbf[:, C_IDENT:C_IDENT + 128], rhs=cbf[:, C_NEGW + 384 - 128 * o:C_NEGW + 896 - 128 * o],
                                start=False, stop=True)

            def S0(i):
                qg_, kb = steps[i]
                o = kb - 4 * qg_
                ps, B_ps = p1.next()
                st[i]["p1"] = (ps, B_ps)
                P.op("pe", lambda e: e.matmul(ps[:], lhsT=k[:, kb * 128:(kb + 1) * 128], rhs=q[:, qg_ * 512:(qg_ + 1) * 512], start=True, stop=(o < 0)),
                     reads=[B_hs], writes=[B_ps])
                if o >= 0:
                    P.op("pe", lambda e: mask_mm(e, ps, o), reads=[B_c], writes=[B_ps])

            def S1(i):
                ps, B_ps = st[i]["p1"]
                ee, B_e = er.next()
                st[i]["e"] = (ee, B_e)
                P.op("act", lambda e: e.activation(out=ee[:], in_=ps[:], func=AF.Exp), reads=[B_ps], writes=[B_e])
                sp, B_sp = spr.next()
                st[i]["sp"] = (sp, B_sp)
                P.op("act", lambda e: e.activation(out=sp[:], in_=ee[:], func=AF.Ln, bias=1.0, scale=1.0), reads=[B_e], writes=[B_sp])

            def S2(i):
                qg_, kb = steps[i]
                first = (kb == 4 * qg_ + 3)
                sp, B_sp = st[i]["sp"]
                ps, B_ps = p2.next()
                st[i]["p2"] = (ps, B_ps)
                if not first:
                    lb, B_lb = st[i]["lb"]
                    P.op("pe", lambda e: e.matmul(ps[:], lhsT=cbf[:, C_NONES:C_NONES + 128], rhs=lb[:], start=True, stop=False),
                         reads=[B_c, B_lb], writes=[B_ps])
                P.op("pe", lambda e: e.matmul(ps[:], lhsT=cbf[:, C_UNEG:C_UNEG + 128], rhs=sp[:], start=first, stop=True),
                     reads=[B_c, B_sp], writes=[B_ps])
                if kb > 0:
                    lb2, B_lb2 = lbr.next()
                    if first:
                        P.op("dve", lambda e: e.tensor_copy(out=lb2[:], in_=sp[:]), reads=[B_sp], writes=[B_lb2])
                        P.op("dve", lambda e: e.tensor_copy(out=lsum[:], in_=sp[:]), reads=[B_sp], writes=[B_lsum])
                    else:
                        P.op("dve", lambda e: e.tensor_tensor(out=lb2[:], in0=lsum[:], in1=sp[:], op=ALU.add), reads=[B_sp, B_lsum], writes=[B_lb2])
                        if kb > 1:
                            P.op("dve", lambda e: e.tensor_tensor(out=lsum[:], in0=lsum[:], in1=sp[:], op=ALU.add), reads=[B_sp, B_lsum], writes=[B_lsum])
                    st[i + 1]["lb"] = (lb2, B_lb2)

            def S3(i):
                ps, B_ps = st[i]["p2"]
                ee, B_e = st[i]["e"]
                ec, B_ec = ecr.next()
                P.op("act", lambda e: e.activation(out=ec[:], in_=ps[:], func=AF.Exp), reads=[B_ps], writes=[B_ec])
                aa, B_a = ar.next()
                st[i]["a"] = (aa, B_a)
                P.op("dve", lambda e: e.tensor_tensor(out=aa[:], in0=ee[:], in1=ec[:], op=ALU.mult), reads=[B_e, B_ec], writes=[B_a])

            def S4(i):
                qg_, kb = steps[i]
                first = (kb == 4 * qg_ + 3)
                aa, B_a = st[i]["a"]
                if first:
                    cur_o[0] = po.next()
                o_ps, B_o = cur_o[0]
                P.op("pe", lambda e: e.matmul(o_ps[:], lhsT=v[:, kb, :], rhs=aa[:], start=first, stop=(kb == 0)),
                     reads=[B_hs, B_a], writes=[B_o])
                if kb == 0:
                    y, B_y = yo.next()
                    P.op("dve", lambda e: e.tensor_copy(out=y[:], in_=o_ps[:]), reads=[B_o], writes=[B_y])
                    P.dma(st_eng, lambda e: e.dma_start(out=bY[256 + h * 128:256 + (h + 1) * 128, qg_ * 512:(qg_ + 1) * 512], in_=y[:]), reads=[B_y])
                st[i].clear()

            stages = [S0, S1, S2, S3, S4]
            for tick in range(n + 4):
                for si, fn in enumerate(stages):
                    i = tick - si
                    if 0 <= i < n:
                        fn(i)
            head_done(2 + h, yo.bufs)

    load_head(0)
    for hi_, (kind, a, b) in enumerate(heads):
        if hi_ + 1 < len(heads):
            load_head(hi_ + 1)
        q, k, v, B_hs = HS[hi_ % 2]
        run_head(hi_, kind, a, b, q, k, v, B_hs)


def phase_C(K, TOK, xT, xT_out, bYg, yloc, gT, wpd, wps, wpo, wp1, wp2, n2g, st_eng="pool"):
    P = K.P
    nt = TOK // T
    u = K.uid
    K.uid += 1
    wring = K.ring("wringC", [128, 8192], BF16, 3)
    ws = WStream(K, wring, 2)
    for t in range(nt):
        ws.add(wpd, 0, 0)
        ws.add(wps, 0, 0)
        ws.add(wps, 0, 1)
        for cb in range(4):
            ws.add(wpo, 0, cb)
        for half in range(2):
            for cb in range(8):
                ws.add(wp1, 0, half * 8 + cb)
            for cb in range(8):
                ws.add(wp2, half, cb)
    B_yloc = P.buf("yloc")
    for rk in range(2):
        P.dma("sp", lambda e, rk=rk: e.dma_start(out=yloc[rk * 768:(rk + 1) * 768, :].rearrange("(c x) t -> c x t", x=128),
                                                 in_=K.dynview(e, "bYg", lambda hf: bYg[:, :, :, bass.DynSlice(hf * TOK, TOK)])[:, rk]), writes=[B_yloc])
    par = K.sb("parC%d" % u, [128, 16], F32); B_par = P.buf("parC")
    P.dma("sp", lambda e: e.dma_start(out=par[:], in_=n2g), writes=[B_par])
    x = K.sb("xC%d" % u, [128, 16, T], F32)
    B_x = [P.buf("xC%d" % i) for i in range(16)]
    yd = K.sb("ydC%d" % u, [128, 4, T], BF16); B_yd = P.buf("yd")
    ys = K.sb("ysC%d" % u, [128, 8, T], BF16); B_ys = P.buf("ys")
    gr = K.ring("gC%d" % u, [128, 2, T], BF16, 3)
    mixed = K.sb("mixC%d" % u, [128, 16, T], BF16)
    B_mix = [P.buf("mix%d" % i) for i in range(16)]
    h2 = K.sb("h2C%d" % u, [128, 16, T], BF16)
    B_h2 = [P.buf("h2%d" % i) for i in range(16)]
    fT = K.sb("fC%d" % u, [128, 32, T], BF16)
    B_f = [P.buf("f%d" % i) for i in range(32)]
    acc = K.ring("accC%d" % u, [128, T], F32, 4, psum=True)
    ss_ps = K.ps("ssC%d" % u, [128, T]); B_ss = P.buf("ssC")
    t1r = K.ring("t1C%d" % u, [128, T], F32, 2)
    t2r = K.ring("t2C%d" % u, [128, T], F32, 2)
    sqr = K.ring("sqC%d" % u, [128, T], BF16, 3)
    rstd = K.sb("rstdC%d" % u, [128, T], F32); B_rstd = P.buf("rstdC")
    lnt = K.sb("lntC%d" % u, [128, T], F32); B_lnt = P.buf("lntC")
    cbf = K.cbf
    B_c = K.B_c
    B_out = P.buf("xout")
    wi = 0
    for t in range(nt):
        t0 = t * T
        for kc in range(16):
            P.dma("sp", lambda e, kc=kc, t0=t0: e.dma_start(out=x[:, kc, :], in_=xT[kc * 128:(kc + 1) * 128, t0:t0 + T]), writes=[B_x[kc]])
        for rk in range(2):
            P.dma("sp", lambda e, t0=t0, rk=rk: e.dma_start(
                out=yd[:, 2 * rk:2 * rk + 2, :],
                in_=yloc[rk * 768:rk * 768 + 256, t0:t0 + T].rearrange("(kc p) t -> p kc t", p=128)), reads=[B_yloc], writes=[B_yd])
            P.dma("sp", lambda e, t0=t0, rk=rk: e.dma_start(
                out=ys[:, 4 * rk:4 * rk + 4, :],
                in_=yloc[rk * 768 + 256:rk * 768 + 768, t0:t0 + T].rearrange("(kc p) t -> p kc t", p=128)), reads=[B_yloc], writes=[B_ys])
        wd_t, B_wd = ws.get(wi, 2); wi += 1
        ws_t = []
        for i in range(2):
            ws_t.append(ws.get(wi, 1 - i)); wi += 1
        for c in range(16):
            gt, B_g = gr.next()
            P.dma("sp", lambda e, gt=gt, c=c, t0=t0: e.dma_start(out=gt[:, 0, :], in_=gT[c, :, t0:t0 + T]), writes=[B_g])
            P.dma("sp", lambda e, gt=gt, c=c, t0=t0: e.dma_start(out=gt[:, 1, :], in_=gT[16 + c, :, t0:t0 + T]), writes=[B_g])
            pa, B_pa = acc.next()
            for kc in range(4):
                P.op("pe", lambda e, pa=pa, kc=kc, c=c, wd_t=wd_t: e.matmul(pa[:], lhsT=wd_t[:, kc, c * 128:(c + 1) * 128], rhs=yd[:, kc, :], start=(kc == 0), stop=(kc == 3)),
                     reads=[B_wd, B_yd], writes=[B_pa])
            pb, B_pb = acc.next()
            wst, B_wst = ws_t[c // 8]
            cc = c % 8
            for kc in range(8):
                P.op("pe", lambda e, pb=pb, kc=kc, cc=cc, wst=wst: e.matmul(pb[:], lhsT=wst[:, kc, cc * 128:(cc + 1) * 128], rhs=ys[:, kc, :], start=(kc == 0), stop=(kc == 7)),
                     reads=[B_wst, B_ys], writes=[B_pb])
            t1, B_t1 = t1r.next()
            P.op("dve", lambda e, t1=t1, pa=pa, gt=gt: e.tensor_tensor(out=t1[:], in0=pa[:], in1=gt[:, 0, :], op=ALU.mult), reads=[B_pa, B_g], writes=[B_t1])
            t2, B_t2 = t2r.next()
            P.op("dve", lambda e, t2=t2, pb=pb, gt=gt: e.tensor_tensor(out=t2[:], in0=pb[:], in1=gt[:, 1, :], op=ALU.mult), reads=[B_pb, B_g], writes=[B_t2])
            P.op("pool", lambda e, t1=t1, t2=t2, c=c: e.tensor_tensor(out=mixed[:, c, :], in0=t1[:], in1=t2[:], op=ALU.add), reads=[B_t1, B_t2], writes=[B_mix[c]])
        for cb in range(4):
            wt, B_w = ws.get(wi); wi += 1
            for ci in range(4):
                c = cb * 4 + ci
                pa, B_pa = acc.next()
                for kc in range(16):
                    P.op("pe", lambda e, pa=pa, kc=kc, ci=ci, wt=wt: e.matmul(pa[:], lhsT=wt[:, kc, ci * 128:(ci + 1) * 128], rhs=mixed[:, kc, :], start=(kc == 0), stop=(kc == 15)),
                         reads=[B_w, B_mix[kc]], writes=[B_pa])
                P.op("dve", lambda e, pa=pa, c=c: e.tensor_tensor(out=x[:, c, :], in0=pa[:], in1=x[:, c, :], op=ALU.add), reads=[B_pa, B_x[c]], writes=[B_x[c]])
                sq, B_sq = sqr.next()
                P.op("act", lambda e, sq=sq, c=c: e.activation(out=sq[:], in_=x[:, c, :], func=AF.Square), reads=[B_x[c]], writes=[B_sq])
                P.op("pool", lambda e, c=c: e.tensor_scalar(out=h2[:, c, :], in0=x[:, c, :], scalar1=par[:, c:c + 1], scalar2=None, op0=ALU.mult),
                     reads=[B_x[c], B_par], writes=[B_h2[c]])
                P.op("pe", lambda e, sq=sq, c=c: e.matmul(ss_ps[:], lhsT=cbf[:, C_ONES:C_ONES + 128], rhs=sq[:], start=(c == 0), stop=(c == 15)),
                     reads=[B_sq, B_c], writes=[B_ss])
        rstd_from_ss(K, ss_ps[:], B_ss, rstd[:], B_rstd, lnt[:], B_lnt, 1.0 / D)
        for half in range(2):
            for cb in range(8):
                wt, B_w = ws.get(wi); wi += 1
                for ci in range(4):
                    fc = cb * 4 + ci
                    pa, B_pa = acc.next()
                    for kc in range(16):
                        P.op("pe", lambda e, pa=pa, kc=kc, ci=ci, wt=wt: e.matmul(pa[:], lhsT=wt[:, kc, ci * 128:(ci + 1) * 128], rhs=h2[:, kc, :], start=(kc == 0), stop=(kc == 15)),
                             reads=[B_w, B_h2[kc]], writes=[B_pa])
                    t1, B_t1 = t1r.next()
                    P.op("dve", lambda e, t1=t1, pa=pa: e.scalar_tensor_tensor(out=t1[:], in0=pa[:], scalar=0.0, in1=rstd[:], op0=ALU.max, op1=ALU.mult),
                         reads=[B_pa, B_rstd], writes=[B_t1])
                    P.op("act", lambda e, t1=t1, fc=fc: e.activation(out=fT[:, fc, :], in_=t1[:], func=AF.Square), reads=[B_t1], writes=[B_f[fc]])
            for cb in range(8):
                wt, B_w = ws.get(wi); wi += 1
                wt2 = wt
                for ci in range(2):
                    c = cb * 2 + ci
                    pa, B_pa = acc.next()
                    for kc in range(32):
                        P.op("pe", lambda e, pa=pa, kc=kc, ci=ci, wt2=wt2: e.matmul(pa[:], lhsT=wt2[:, kc, ci * 128:(ci + 1) * 128], rhs=fT[:, kc, :], start=(kc == 0), stop=(kc == 31)),
                             reads=[B_w, B_f[kc]], writes=[B_pa])
                    P.op("dve", lambda e, pa=pa, c=c: e.tensor_tensor(out=x[:, c, :], in0=pa[:], in1=x[:, c, :], op=ALU.add), reads=[B_pa, B_x[c]], writes=[B_x[c]])
                    if half == 1:
                        P.dma(st_eng, lambda e, c=c, t0=t0: e.dma_start(out=xT_out[c * 128:(c + 1) * 128, t0:t0 + T], in_=x[:, c, :]),
                              reads=[B_x[c]], writes=[B_out], sem_of=B_x[c])


def make_wprep_in(K, l, w):
    return {"in": WPrep(K, "in%d" % l, w["w_in"][l], D, NIN, 16, 512, ngroups=(4 if l == 0 else 1))}


def make_wpreps_c(K, l, w):
    return {
        "ud": WPrep(K, "ud%d" % l, w["w_ud"][l], 512, D, 4, 2048),
        "us": WPrep(K, "us%d" % l, w["w_us"][l], 1024, D, 8, 1024),
        "o": WPrep(K, "o%d" % l, w["w_o"][l], D, D, 16, 512),
        "f1": WPrep(K, "f1%d" % l, w["w1"][l], D, DFF, 16, 512),
        "f2": WPrep(K, "f2%d" % l, w["w2"][l], DFF, D, 32, 256),
    }


def build_fused(S):
    TOK = S // 2
    nc = bass.Bass("TRN2", target_bir_lowering=False)
    with ExitStack() as es:
        K = KB(nc, es)
        P = K.P
        xT = K.dram("xT", [D, TOK], F32, "ExternalInput")
        w = {
            "w_in": K.dram("w_in", [DEPTH, D, NIN], F32, "ExternalInput"),
            "w_ud": K.dram("w_ud", [DEPTH, 512, D], F32, "ExternalInput"),
            "w_us": K.dram("w_us", [DEPTH, 1024, D], F32, "ExternalInput"),
            "w_o": K.dram("w_o", [DEPTH, D, D], F32, "ExternalInput"),
            "w1": K.dram("w1", [DEPTH, D, DFF], F32, "ExternalInput"),
            "w2": K.dram("w2", [DEPTH, DFF, D], F32, "ExternalInput"),
        }
        n1g = K.dram("n1g", [DEPTH, 128, 16], F32, "ExternalInput")
        n2g = K.dram("n2g", [DEPTH, 128, 16], F32, "ExternalInput")
        qg = K.dram("qg", [DEPTH, 128, 3], F32, "ExternalInput")
        kg = K.dram("kg", [DEPTH, 128, 3], F32, "ExternalInput")
        gb = K.dram("gb", [DEPTH, 128, 32], F32, "ExternalInput")
        cosT = K.dram("cosT", [128, TOK], F32, "ExternalInput")
        sinT = K.dram("sinT", [128, TOK], F32, "ExternalInput")
        cbf = K.dram("cbf", [128, NCBF], BF16, "ExternalInput")
        cf32 = K.dram("cf32", [128, NCF32], F32, "ExternalInput")
        xo = K.dram("xT_out", [D, TOK], F32, "ExternalOutput")
        xmid = K.dram("xT_mid", [D, TOK], F32)
        NT = TOK // T
        aT = K.dram("aT", [NT, 5120, T], BF16)
        aV = K.dram("aV", [TOK, 2560], BF16)
        gT = K.dram("gT", [32, 128, TOK], BF16)
        aTg = K.dram("aTg", [NT, 4, 2, 1280, T], BF16)
        aVg = K.dram("aVg", [TOK // 256, 2, 256, 2560], BF16)
        bY = K.dram("bY", [768, S], BF16)
        bYg = K.dram("bYg", [6, 2, 128, S], BF16)
        vloc = K.dram("vloc", [S, 1280], BF16)
        tloc = K.dram("tloc", [2, 2560, TOK], BF16)
        yloc = K.dram("yloc", [2 * 768, TOK], BF16)
        groups = [[0, 1], [2, 3], [4, 5], [6, 7]]
        B_cc = P.buf("cc")

        def ag(src, dst, after):
            P.dma("pool", lambda e: e.collective_compute("AllGather", ALU.bypass, replica_groups=groups, ins=[src], outs=[dst]),
                  writes=[B_cc], inc=1, after=after)

        def tile_done(t, stage_bufs):
            for j in range(4):
                ag(aT[t, j * 1280:(j + 1) * 1280, :], aTg[t, j].rearrange("r x c -> (r x) c"), stage_bufs)
            for c in range(2 * t, 2 * t + 2):
                ag(aV[c * 256:(c + 1) * 256, :], aVg[c].rearrange("r x d -> (r x) d"), stage_bufs)

        def head_done(c, stage_bufs):
            ag(bY[c * 128:(c + 1) * 128, :], bYg[c].rearrange("r x t -> (r x) t"), stage_bufs)

        K.load_consts(cbf, cf32)
        wps = make_wprep_in(K, 0, w)
        for l in range(DEPTH):
            x_in = xT if l == 0 else xmid
            x_out = xmid if l == 0 else xo
            phase_A(K, TOK, x_in, n1g[l], wps["in"], qg[l], kg[l], gb[l], cosT, sinT, aT, aV, gT, tile_done)
            K.new_phase()
            wps.update(make_wpreps_c(K, l, w))
            if l + 1 < DEPTH:
                wps_next = make_wprep_in(K, l + 1, w)
            phase_B(K, S, aTg, aVg, bY, vloc, tloc, head_done)
            K.new_phase()
            phase_C(K, TOK, x_in, x_out, bYg, yloc, gT, wps["ud"], wps["us"], wps["o"], wps["f1"], wps["f2"], n2g[l])
            K.new_phase()
            if l + 1 < DEPTH:
                wps = wps_next
        P.run(nc)
    return nc


_CACHE = {}


def fm(v):
    return np.ascontiguousarray(v.reshape(v.shape[0], -1, 128).transpose(0, 2, 1))


def run_model(x, norm1_g, w_in, q_norm_g, k_norm_g, w_up_dil, w_up_sb, gate_b, w_out, norm2_g, w_ff1, w_ff2):
    Bn, S, _ = x.shape
    TOK = S // 2
    cbf, cf32 = make_consts()
    cosT, sinT = rope_tables(S)
    cores = list(range(NCORES))
    if S not in _CACHE:
        _CACHE[S] = build_fused(S)
    nc = _CACHE[S]
    shared = {
        "w_in": np.ascontiguousarray(w_in), "w_ud": np.ascontiguousarray(w_up_dil), "w_us": np.ascontiguousarray(w_up_sb),
        "w_o": np.ascontiguousarray(w_out), "w1": np.ascontiguousarray(w_ff1), "w2": np.ascontiguousarray(w_ff2),
        "n1g": fm(norm1_g), "n2g": fm(norm2_g),
        "qg": np.ascontiguousarray(q_norm_g.transpose(0, 2, 1)), "kg": np.ascontiguousarray(k_norm_g.transpose(0, 2, 1)),
        "gb": fm(gate_b.reshape(DEPTH, -1)), "cbf": cbf, "cf32": cf32,
    }
    maps = []
    for c in cores:
        b, hf = c // 2, c % 2
        m = dict(shared)
        m["xT"] = np.ascontiguousarray(x[b, hf * TOK:(hf + 1) * TOK, :].T)
        m["cosT"] = np.ascontiguousarray(cosT[:, hf * TOK:(hf + 1) * TOK])
        m["sinT"] = np.ascontiguousarray(sinT[:, hf * TOK:(hf + 1) * TOK])
        maps.append(m)
    res = run_bass_kernel_spmd(nc, maps, core_ids=cores).results
    out = np.empty((Bn, S, D), np.float32)
    for c in cores:
        out[c // 2, (c % 2) * TOK:(c % 2 + 1) * TOK, :] = res[c]["xT_out"].T
    return out


def kernel(x, norm1_g, w_in, q_norm_g, k_norm_g, w_up_dil, w_up_sb, gate_b, w_out, norm2_g, w_ff1, w_ff2):
    a = [np.asarray(v, dtype=np.float32) for v in
         (x, norm1_g, w_in, q_norm_g, k_norm_g, w_up_dil, w_up_sb, gate_b, w_out, norm2_g, w_ff1, w_ff2)]
    return run_model(*a)
```

```python
import math
from contextlib import ExitStack
import numpy as np
import ml_dtypes
import concourse.bass as bass
import concourse.mybir as mybir
from concourse.bass_utils import run_bass_kernel_spmd

F32 = mybir.dt.float32
BF16 = mybir.dt.bfloat16
AF = mybir.ActivationFunctionType
ALU = mybir.AluOpType

D = 2048
DEPTH = 2
HD = 128
NG = 3
DIL = (1, 4, 16)
DILW = 1536
SBW = 1024
NIN = 11776
DFF = 8192
EPS = 1e-6
T = 512
NCORES = 8
NEG = -30000.0
ISQ = 1.0 / math.sqrt(128.0)

ENGS = ("pe", "act", "dve", "pool", "sp")


class Buf:
    __slots__ = ("name", "w", "r", "lsem", "ssem")

    def __init__(self, name):
        self.name = name
        self.w = {}
        self.r = {}
        self.lsem = None
        self.ssem = None


class Prog:
    def __init__(self):
        self.streams = {e: [] for e in ENGS}
        self.phase = 0
        self.count = {e: 0 for e in ENGS}
        self.waited = {e: {} for e in ENGS}
        self.dma_val = {}
        self.n_dma_sems = 0
        self.free_sems = []
        self.used_sems = []

    def buf(self, name=""):
        return Buf(name)

    def ek(self, eng):
        return ("e", self.phase, eng)

    def _new_dma_sem(self):
        if self.free_sems:
            k = self.free_sems.pop()
        else:
            k = ("d", self.n_dma_sems)
            self.n_dma_sems += 1
            self.dma_val[k] = 0
        self.used_sems.append(k)
        return k

    def barrier(self):
        need = {}
        for e in ENGS:
            if self.count[e] > 0:
                need[self.ek(e)] = self.count[e]
        for k, v in self.dma_val.items():
            if v > 0:
                need[k] = v
        for e in ENGS:
            wd = self.waited[e]
            waits = []
            for k, v in need.items():
                if k == self.ek(e):
                    continue
                if wd.get(k, 0) < v:
                    wd[k] = v
                    waits.append((k, v))
            self.streams[e].append((waits, None, None))
        self.phase += 1
        self.count = {e: 0 for e in ENGS}
        self.free_sems.extend(self.used_sems)
        self.used_sems = []

    def _collect(self, eng, reads, writes):
        need = {}
        own = self.ek(eng)
        for b in reads:
            for k, v in b.w.items():
                if need.get(k, 0) < v:
                    need[k] = v
        for b in writes:
            for k, v in b.w.items():
                if need.get(k, 0) < v:
                    need[k] = v
            for k, v in b.r.items():
                if k == own:
                    continue
                if need.get(k, 0) < v:
                    need[k] = v
        wd = self.waited[eng]
        out = []
        for k, v in need.items():
            if k[0] == "e" and k[1] != self.phase:
                continue
            if k == own and eng in ("pe", "sp"):
                continue
            if wd.get(k, 0) < v:
                wd[k] = v
                out.append((k, v))
        return out

    def op(self, eng, fn, reads=(), writes=()):
        waits = self._collect(eng, reads, writes)
        self.count[eng] += 1
        v = self.count[eng]
        k = self.ek(eng)
        self.streams[eng].append((waits, fn, (k, 1)))
        for b in reads:
            b.r[k] = v
        for b in writes:
            b.w[k] = v

    def _sem_for(self, b, attr):
        cur = getattr(b, attr)
        if cur is None or cur[0] != self.phase:
            cur = (self.phase, self._new_dma_sem())
            setattr(b, attr, cur)
        return cur[1]

    def dma(self, eng, fn, reads=(), writes=(), sem_of=None, inc=16, after=()):
        waits = self._collect(eng, reads, list(writes) + list(after))
        if sem_of is not None:
            k = self._sem_for(sem_of, "lsem")
        elif writes:
            k = self._sem_for(writes[0], "lsem")
        else:
            k = self._sem_for(reads[0], "ssem")
        self.dma_val[k] += inc
        v = self.dma_val[k]
        self.streams[eng].append((waits, fn, (k, inc)))
        for b in reads:
            b.r[k] = v
        for b in writes:
            b.w[k] = v

    def run(self, nc):
        with ExitStack() as es:
            sems = {}
            for p in range(self.phase + 1):
                for e in ENGS:
                    sems[("e", p, e)] = es.enter_context(nc.semaphore("s%d_%s" % (p, e)))
            for i in range(self.n_dma_sems):
                sems[("d", i)] = es.enter_context(nc.semaphore("sd%d" % i))
            block = es.enter_context(nc.Block())
            streams = self.streams

            def play(engobj, name):
                for waits, fn, inc in streams[name]:
                    for k, v in waits:
                        engobj.wait_ge(sems[k], v)
                    if fn is not None:
                        fn(engobj).then_inc(sems[inc[0]], inc[1])

            @block.tensor
            def _(e):
                play(e, "pe")

            @block.scalar
            def _(e):
                play(e, "act")

            @block.vector
            def _(e):
                play(e, "dve")

            @block.gpsimd
            def _(e):
                play(e, "pool")

            @block.sync
            def _(e):
                play(e, "sp")


class Ring:
    def __init__(self, K, name, shape, dt, n, psum=False):
        self.tiles = [(K.ps(name + str(i), shape) if psum else K.sb(name + str(i), shape, dt)) for i in range(n)]
        self.bufs = [K.P.buf(name + str(i)) for i in range(n)]
        self.i = 0

    def next(self):
        j = self.i % len(self.tiles)
        self.i += 1
        return self.tiles[j], self.bufs[j]


C_ONES = 0
C_UNEG = 128
C_NONES = 256
C_IDENT = 384
C_NEGW = 512
C_NEGD = 1408
NCBF = 1664
F_ONES = 0
F_PERM = 128
F_INV = 256
NCF32 = 257


def make_consts():
    cb = np.zeros((128, NCBF), np.float32)
    j = np.arange(128)[:, None]
    s = np.arange(128)[None, :]
    cb[:, C_ONES:C_ONES + 128] = 1.0
    cb[:, C_UNEG:C_UNEG + 128] = np.where(j >= s, -1.0, 0.0)
    cb[:, C_NONES:C_NONES + 128] = -1.0
    cb[:, C_IDENT:C_IDENT + 128] = np.eye(128)
    c = np.arange(896)[None, :]
    cb[:, C_NEGW:C_NEGW + 896] = np.where((c - 384) <= j, NEG, 0.0)
    cb[:, C_NEGD:C_NEGD + 128] = np.where(j >= s, 0.0, NEG)
    cb[:, C_NEGD + 128:C_NEGD + 256] = np.where(j <= s, 0.0, NEG)
    cf = np.zeros((128, NCF32), np.float32)
    cf[:, F_ONES:F_ONES + 128] = 1.0
    cf[:, F_PERM:F_PERM + 128] = (j == ((s + 64) % 128)).astype(np.float32)
    cf[:, F_INV] = 1.0 / 128.0
    return cb.astype(ml_dtypes.bfloat16), cf


def rope_tables(S):
    half = HD // 2
    inv_freq = (np.float32(10000.0) ** (-np.arange(half, dtype=np.float32) / np.float32(half))).astype(np.float32)
    ang = (np.arange(S, dtype=np.float32)[:, None] * inv_freq[None, :]).astype(np.float32)
    cos = np.cos(ang).astype(np.float32)
    sin = np.sin(ang).astype(np.float32)
    cosT = np.concatenate([cos, cos], axis=1).T
    sinT = np.concatenate([-sin, sin], axis=1).T
    return np.ascontiguousarray(cosT), np.ascontiguousarray(sinT)


ARENA_W = 192 * 256


class KB:
    def __init__(self, nc, es):
        self.nc = nc
        self.es = es
        self.P = Prog()
        self.uid = 0
        self.arena = es.enter_context(nc.sbuf_tensor("arena", [128, ARENA_W], F32))
        self.banks = [es.enter_context(nc.psum_tensor("bank%d" % i, [128, 512], F32)) for i in range(8)]
        self.base = 0
        self.off = 0
        self.bank_i = 0

    def new_phase(self):
        self.P.barrier()
        self.off = self.base
        self.bank_i = 0

    def sb(self, name, shape, dt):
        n = 1
        for d in shape[1:]:
            n *= d
        esz = 4 if dt == F32 else 2
        words = (n * esz + 3) // 4
        words = (words + 7) // 8 * 8
        assert self.off + words <= ARENA_W, ("arena overflow", name, self.off, words)
        ap = self.arena[:, self.off:self.off + words]
        self.off += words
        if dt != F32:
            ap = ap.bitcast(dt)
        ap = ap[:, 0:n]
        if len(shape) == 3:
            ap = ap.rearrange("p (a b) -> p a b", a=shape[1])
        return ap

    def ps(self, name, shape, dt=F32):
        assert self.bank_i < 8, "psum overflow"
        b = self.banks[self.bank_i]
        self.bank_i += 1
        return b[:, 0:shape[1]]

    def dynview(self, e, key, mk):
        if not hasattr(self, "_dv"):
            self._dv = {}
            self._hf = e.partition_id() % 2
        if key not in self._dv:
            self._dv[key] = mk(self._hf)
        return self._dv[key]

    def dram(self, name, shape, dt, kind="Internal"):
        return self.nc.dram_tensor(name, shape, dt, kind=kind).ap()

    def ring(self, name, shape, dt, n, psum=False):
        return Ring(self, name, shape, dt, n, psum)

    def load_consts(self, cbf_ap, cf32_ap):
        self.cbf = self.sb("cbf_sb", [128, NCBF], BF16)
        self.cf = self.sb("cf32_sb", [128, NCF32], F32)
        self.base = self.off
        self.B_c = self.P.buf("consts")
        self.P.dma("sp", lambda e: e.dma_start(out=self.cbf[:], in_=cbf_ap), writes=[self.B_c])
        self.P.dma("sp", lambda e: e.dma_start(out=self.cf[:], in_=cf32_ap), writes=[self.B_c])


class WPrep:
    def __init__(self, K, name, w_ap, Kdim, N, kcb, ncols, ngroups=1):
        self.kcb, self.ncols = kcb, ncols
        self.nkb = Kdim // (128 * kcb)
        self.ncb = N // ncols
        self.scr = K.dram("wscr_" + name, [self.nkb, self.ncb, 128, kcb * ncols], BF16)
        self.bufs = {}
        wv = w_ap.rearrange("(kc p) n -> p kc n", p=128)
        gb_ = [K.P.buf("w_%s_%d" % (name, i)) for i in range(ngroups)]
        per = -(-(self.nkb * self.ncb) // ngroups)
        for kb in range(self.nkb):
            for cb in range(self.ncb):
                b = gb_[(kb * self.ncb + cb) // per]
                self.bufs[(kb, cb)] = b
                dst = self.scr[kb, cb].rearrange("p (kc n) -> p kc n", kc=kcb)
                src = wv[:, kb * kcb:(kb + 1) * kcb, cb * ncols:(cb + 1) * ncols]
                K.P.dma("pool", lambda e, dst=dst, src=src: e.dma_start(out=dst, in_=src), writes=[b])


class WStream:
    def __init__(self, K, ring, depth):
        self.K, self.ring, self.depth = K, ring, depth
        self.plan = []
        self.loaded = []
        self.emitted = 0

    def add(self, wp, kb, cb):
        self.plan.append((wp, kb, cb))
        return len(self.plan) - 1

    def get(self, j, depth=None):
        hi = min(len(self.plan), j + (self.depth if depth is None else depth) + 1)
        while self.emitted < hi:
            wp, kb, cb = self.plan[self.emitted]
            tile, b = self.ring.next()
            src = wp.scr[kb, cb]
            self.K.P.dma("sp", lambda e, tile=tile, src=src: e.dma_start(out=tile[:], in_=src),
                         reads=[wp.bufs[(kb, cb)]], writes=[b])
            self.loaded.append((tile, b, wp))
            self.emitted += 1
        tile, b, wp = self.loaded[j]
        return tile[:].rearrange("p (kc n) -> p kc n", kc=wp.kcb), b


def rstd_from_ss(K, ss_ps, B_ss, out_tile, B_out, tmp, B_tmp, inv_n):
    K.P.op("act", lambda e: e.activation(out=tmp, in_=ss_ps, func=AF.Ln, bias=EPS, scale=inv_n),
           reads=[B_ss], writes=[B_tmp])
    K.P.op("act", lambda e: e.activation(out=out_tile, in_=tmp, func=AF.Exp, scale=-0.5),
           reads=[B_tmp], writes=[B_out])


def phase_A(K, TOK, xT, n1g, wp, qg, kg, gb, cosT, sinT, aT, aV, gT, tile_done, st_eng="pool"):
    P = K.P
    nt = TOK // T
    wring = K.ring("wringA", [128, 8192], BF16, 3)
    ws = WStream(K, wring, 2)
    order = [0, 1, 2, 3, 4, 5, 9, 10, 11, 12, 15, 16, 17, 18, 19, 20, 21, 22, 6, 7, 8, 13, 14]
    for t in range(nt):
        for cb in order:
            ws.add(wp, 0, cb)
    par = K.sb("parA%d" % K.uid, [128, 16 + 3 + 3 + 32], F32)
    B_par = P.buf("parA")
    P.dma("sp", lambda e: e.dma_start(out=par[:, 0:16], in_=n1g), writes=[B_par])
    P.dma("sp", lambda e: e.dma_start(out=par[:, 16:19], in_=qg), writes=[B_par])
    P.dma("sp", lambda e: e.dma_start(out=par[:, 19:22], in_=kg), writes=[B_par])
    P.dma("sp", lambda e: e.dma_start(out=par[:, 22:54], in_=gb), writes=[B_par])

    hT = K.sb("hT%d" % K.uid, [128, 16, T], BF16)
    B_h = [P.buf("h%d" % i) for i in range(16)]
    xr = K.ring("xr%d" % K.uid, [128, T], F32, 3)
    sqr = K.ring("sqr%d" % K.uid, [128, T], BF16, 3)
    rstd = K.sb("rstd%d" % K.uid, [128, T], F32); B_rstd = P.buf("rstd")
    lnt = K.sb("lnt%d" % K.uid, [128, T], F32); B_lnt = P.buf("lnt")
    rstdT = K.sb("rstdT%d" % K.uid, [128, 4], F32); B_rstdT = P.buf("rstdT")
    cs = K.sb("cs%d" % K.uid, [128, 2, T], F32); B_cs = P.buf("cs")
    acc = K.ring("accA%d" % K.uid, [128, T], F32, 3, psum=True)
    ss_ps = K.ps("ssA%d" % K.uid, [128, T]); B_ss = P.buf("ssA")
    s2_ps = K.ps("s2A%d" % K.uid, [128, T]); B_s2 = P.buf("s2A")
    pm_ps = K.ps("pmA%d" % K.uid, [128, T]); B_pm = P.buf("pmA")
    rt_ps = K.ps("rtA%d" % K.uid, [128, 4]); B_rt = P.buf("rtA")
    tr = K.ring("tA%d" % K.uid, [128, T], F32, 2)
    sq2 = K.ring("sq2A%d" % K.uid, [128, T], F32, 4)
    l2 = K.ring("l2A%d" % K.uid, [128, T], F32, 2)
    tn = K.ring("tnA%d" % K.uid, [128, T], F32, 2)
    ra = K.ring("raA%d" % K.uid, [128, T], F32, 2)
    rb = K.ring("rbA%d" % K.uid, [128, T], F32, 2)
    ob = K.ring("obA%d" % K.uid, [128, T], BF16, 16)
    K.uid += 1
    cbf, cf = K.cbf, K.cf
    B_c = K.B_c
    wi = 0
    pipe = []

    def advance(new=None):
        for ent in list(pipe):
            ent[1][ent[0]]()
            ent[0] += 1
        if new is not None:
            new[0]()
            pipe.append([1, new])
        pipe[:] = [en for en in pipe if en[0] < 3]

    for t in range(nt):
        t0 = t * T
        P.dma("sp", lambda e, t0=t0: e.dma_start(out=cs[:, 0, :], in_=cosT[:, t0:t0 + T]), writes=[B_cs])
        P.dma("sp", lambda e, t0=t0: e.dma_start(out=cs[:, 1, :], in_=sinT[:, t0:t0 + T]), writes=[B_cs])
        for kc in range(16):
            xt, B_x = xr.next()
            P.dma("sp", lambda e, xt=xt, kc=kc, t0=t0: e.dma_start(out=xt[:], in_=xT[kc * 128:(kc + 1) * 128, t0:t0 + T]),
                  writes=[B_x])
            sq, B_sq = sqr.next()
            P.op("act", lambda e, sq=sq, xt=xt: e.activation(out=sq[:], in_=xt[:], func=AF.Square), reads=[B_x], writes=[B_sq])
            P.op("dve", lambda e, xt=xt, kc=kc: e.tensor_scalar(out=hT[:, kc, :], in0=xt[:], scalar1=par[:, kc:kc + 1], scalar2=None,
                                                               op0=ALU.mult), reads=[B_x, B_par], writes=[B_h[kc]])
            P.op("pe", lambda e, sq=sq, kc=kc: e.matmul(ss_ps[:], lhsT=cbf[:, C_ONES:C_ONES + 128], rhs=sq[:],
                                                        start=(kc == 0), stop=(kc == 15)), reads=[B_sq, B_c], writes=[B_ss])
        rstd_from_ss(K, ss_ps[:], B_ss, rstd[:], B_rstd, lnt[:], B_lnt, 1.0 / D)
        for j in range(4):
            P.op("pe", lambda e, j=j: e.matmul(rt_ps[:, j:j + 1], lhsT=rstd[:, j * 128:(j + 1) * 128], rhs=cf[:, F_INV:F_INV + 1],
                                               start=True, stop=True), reads=[B_rstd, B_c], writes=[B_rt])
        P.op("dve", lambda e: e.tensor_copy(out=rstdT[:], in_=rt_ps[:]), reads=[B_rt], writes=[B_rstdT])

        for cb in order:
            wt, B_w = ws.get(wi)
            wi += 1
            if cb in (6, 7, 8, 13, 14):
                for j in range(4):
                    ps, B_ps = acc.next()
                    for kc in range(16):
                        P.op("pe", lambda e, ps=ps, wt=wt, kc=kc, j=j: e.matmul(ps[:], lhsT=hT[:, kc, j * 128:(j + 1) * 128], rhs=wt[:, kc, :],
                                                                                 start=(kc == 0), stop=(kc == 15)),
                             reads=[B_h[kc], B_w], writes=[B_ps])
                    o, B_o = ob.next()
                    P.op("act", lambda e, o=o, ps=ps, j=j: e.activation(out=o[:], in_=ps[:], func=AF.Identity, scale=rstdT[:, j:j + 1]),
                         reads=[B_ps, B_rstdT], writes=[B_o])
                    r0 = t0 + j * 128
                    if cb < 9:
                        g_ = cb - 6
                        for shh in range(2):
                            dc = shh * 1280 + g_ * 256
                            P.dma(st_eng, lambda e, o=o, r0=r0, dc=dc, shh=shh: e.dma_start(out=aV[r0:r0 + 128, dc:dc + 256], in_=o[:, shh * 256:(shh + 1) * 256]),
                                  reads=[B_o])
                    else:
                        dc = (cb - 13) * 1280 + 768
                        P.dma(st_eng, lambda e, o=o, r0=r0, dc=dc: e.dma_start(out=aV[r0:r0 + 128, dc:dc + 512], in_=o[:]), reads=[B_o])
                continue
            for ci in range(4):
                ch = cb * 4 + ci
                ps, B_ps = acc.next()
                for kc in range(16):
                    P.op("pe", lambda e, ps=ps, wt=wt, kc=kc, ci=ci: e.matmul(ps[:], lhsT=wt[:, kc, ci * 128:(ci + 1) * 128], rhs=hT[:, kc, :],
                                                                               start=(kc == 0), stop=(kc == 15)),
                         reads=[B_h[kc], B_w], writes=[B_ps])
                o, B_o = ob.next()
                if ch < 24:
                    isq = ch < 12
                    hidx = ch if isq else ch - 12
                    g = hidx // 4
                    gcol = (16 if isq else 19) + g
                    c0 = ISQ if isq else 1.0
                    slot_ = hidx % 4
                    rb_ = (slot_ // 2) * 20 + (0 if isq else 6) + g * 2 + slot_ % 2
                    ob.i -= 1

                    def mk(ps=ps, B_ps=B_ps, gcol=gcol, c0=c0, rb_=rb_, t0=t0):
                        d = {}

                        def E1():
                            tt, B_t = tr.next()
                            d["t"] = (tt, B_t)
                            P.op("dve", lambda e: e.tensor_tensor(out=tt[:], in0=ps[:], in1=rstd[:], op=ALU.mult),
                                 reads=[B_ps, B_rstd], writes=[B_t])
                            s2, B_q2 = sq2.next()
                            d["s2"] = (s2, B_q2)
                            P.op("act", lambda e: e.activation(out=s2[:], in_=tt[:], func=AF.Square), reads=[B_t], writes=[B_q2])

                        def E2():
                            tt, B_t = d["t"]
                            s2, B_q2 = d["s2"]
                            P.op("pe", lambda e: e.matmul(s2_ps[:], lhsT=cf[:, F_ONES:F_ONES + 128], rhs=s2[:], start=True, stop=True),
                                 reads=[B_q2, B_c], writes=[B_s2])
                            ll, B_l = l2.next()
                            r2, B_r2 = sq2.next()
                            rstd_from_ss(K, s2_ps[:], B_s2, r2[:], B_r2, ll[:], B_l, 1.0 / HD)
                            nn, B_n = tn.next()
                            d["n"] = (nn, B_n)
                            P.op("dve", lambda e: e.scalar_tensor_tensor(out=nn[:], in0=tt[:], scalar=par[:, gcol:gcol + 1],
                                                                         in1=r2[:], op0=ALU.mult, op1=ALU.mult),
                                 reads=[B_t, B_r2, B_par], writes=[B_n])

                        def E3():
                            nn, B_n = d["n"]
                            P.op("pe", lambda e: e.matmul(pm_ps[:], lhsT=cf[:, F_PERM:F_PERM + 128], rhs=nn[:], start=True, stop=True),
                                 reads=[B_n, B_c], writes=[B_pm])
                            aa, B_a = ra.next()
                            P.op("dve", lambda e: e.scalar_tensor_tensor(out=aa[:], in0=nn[:], scalar=c0, in1=cs[:, 0, :],
                                                                         op0=ALU.mult, op1=ALU.mult),
                                 reads=[B_n, B_cs], writes=[B_a])
                            bb, B_b = rb.next()
                            P.op("dve", lambda e: e.scalar_tensor_tensor(out=bb[:], in0=pm_ps[:], scalar=c0, in1=cs[:, 1, :],
                                                                         op0=ALU.mult, op1=ALU.mult),
                                 reads=[B_pm, B_cs], writes=[B_b])
                            o, B_o = ob.next()
                            P.op("dve", lambda e: e.tensor_tensor(out=o[:], in0=aa[:], in1=bb[:], op=ALU.add),
                                 reads=[B_a, B_b], writes=[B_o])
                            P.dma(st_eng, lambda e: e.dma_start(out=aT[t0 // T, rb_ * 128:(rb_ + 1) * 128, :], in_=o[:]), reads=[B_o])

                        return [E1, E2, E3]

                    advance(mk())
                    continue
                advance()
                if ch < 60:
                    isq = ch < 44
                    hidx = ch - 36 if isq else ch - 44
                    rb_ = (hidx // 4) * 20 + (12 if isq else 16) + hidx % 4
                    c0 = ISQ if isq else 1.0
                    P.op("dve", lambda e, o=o, ps=ps, c0=c0: e.scalar_tensor_tensor(out=o[:], in0=ps[:], scalar=c0, in1=rstd[:],
                                                                                    op0=ALU.mult, op1=ALU.mult),
                         reads=[B_ps, B_rstd], writes=[B_o])
                    P.dma(st_eng, lambda e, o=o, rb_=rb_, t0=t0: e.dma_start(out=aT[t0 // T, rb_ * 128:(rb_ + 1) * 128, :], in_=o[:]), reads=[B_o])
                else:
                    gi = ch - 60
                    tt, B_t = tr.next()
                    P.op("dve", lambda e, tt=tt, ps=ps: e.tensor_tensor(out=tt[:], in0=ps[:], in1=rstd[:], op=ALU.mult),
                         reads=[B_ps, B_rstd], writes=[B_t])
                    P.op("act", lambda e, o=o, tt=tt, gi=gi: e.activation(out=o[:], in_=tt[:], func=AF.Sigmoid, bias=par[:, 22 + gi:23 + gi], scale=1.0),
                         reads=[B_t, B_par], writes=[B_o])
                    P.dma(st_eng, lambda e, o=o, gi=gi, t0=t0: e.dma_start(out=gT[gi, :, t0:t0 + T], in_=o[:]), reads=[B_o])
        tile_done(t, ob.bufs)


def phase_B(K, S, aTg, aVg, bY, vloc, tloc, head_done, st_eng="sp"):
    P = K.P
    NB = S // 128
    cbf = K.cbf
    B_c = K.B_c
    HS = []
    for i in range(2):
        q = K.sb("hsq%d" % i, [128, S], BF16)
        k = K.sb("hsk%d" % i, [128, S], BF16)
        v = K.sb("hsv%d" % i, [128, NB, 128], BF16)
        HS.append((q, k, v, P.buf("hs%d" % i)))
    accO = K.sb("accO", [128, S], F32); B_accO = P.buf("accO")
    accD = K.sb("accD", [128, S], F32); B_accD = P.buf("accD")
    heads = []
    for sl in range(2):
        for g in range(NG):
            heads.append(("d", g, sl))
    for h in range(4):
        heads.append(("s", h, 0))

    TOKh = S // 2
    B_vloc = P.buf("vloc")
    B_tloc = P.buf("tloc")
    for rk in range(2):
        for jj in range(2):
            P.dma("sp", lambda e, rk=rk, jj=jj: e.dma_start(
                out=tloc[rk, jj * 1280:(jj + 1) * 1280, :].rearrange("x (t c) -> t x c", c=T),
                in_=K.dynview(e, "aTg", lambda hf: aTg[:, bass.DynSlice(hf * 2, 2)])[:, jj, rk]), writes=[B_tloc])
    for rk in range(2):
        P.dma("sp", lambda e, rk=rk: e.dma_start(out=vloc[rk * TOKh:(rk + 1) * TOKh, :].rearrange("(c x) d -> c x d", x=256),
                                                 in_=K.dynview(e, "aVg", lambda hf: aVg[:, :, :, bass.DynSlice(hf * 1280, 1280)])[:, rk]), writes=[B_vloc])

    def load_head(i):
        kind, a, b = heads[i]
        q, k, v, B = HS[i % 2]
        if kind == "d":
            hi = a * 2 + b
            r = DIL[a]
            nb = NB // r
            qb, kb_, vc = hi, 6 + hi, hi * 128
        else:
            qb, kb_, vc = 12 + a, 16 + a, 768 + a * 128
        for rk in range(2):
            P.dma("sp", lambda e, rk=rk: e.dma_start(out=q[:, rk * TOKh:(rk + 1) * TOKh],
                                                     in_=tloc[rk, qb * 128:(qb + 1) * 128, :]), reads=[B_tloc], writes=[B])
            P.dma("sp", lambda e, rk=rk: e.dma_start(out=k[:, rk * TOKh:(rk + 1) * TOKh],
                                                     in_=tloc[rk, kb_ * 128:(kb_ + 1) * 128, :]), reads=[B_tloc], writes=[B])
        if kind == "d":
            for c in range(r):
                P.dma("sp", lambda e, c=c: e.dma_start(
                    out=v[:, c * nb:(c + 1) * nb, :],
                    in_=vloc[:, vc:vc + 128].rearrange("(n i c) d -> i c n d", i=128, c=r)[:, c]), reads=[B_vloc], writes=[B])
        else:
            P.dma("sp", lambda e: e.dma_start(
                out=v[:], in_=vloc[:, vc:vc + 128].rearrange("(n i) d -> i n d", i=128)), reads=[B_vloc], writes=[B])

    p1 = K.ring("p1", [128, 512], F32, 2, psum=True)
    p2 = K.ring("p2", [128, 512], F32, 2, psum=True)
    po = K.ring("po", [128, 512], F32, 2, psum=True)
    pd = K.ring("pd", [128, 512], F32, 2, psum=True)
    er = K.ring("er", [128, 512], F32, 4)
    ecr = K.ring("ecr", [128, 512], F32, 2)
    spr = K.ring("spr", [128, 512], BF16, 3)
    lsum = K.sb("lsum", [128, 512], F32); B_lsum = P.buf("lsum")
    lbr = K.ring("lbr", [128, 512], BF16, 2)
    ar = K.ring("ar", [128, 512], BF16, 3)
    yo = K.ring("yo", [128, 512], BF16, 2)
    rcp = er

    def run_head(hi_, kind, a, b, q, k, v, B_hs):
        if kind == "d":
            g, sl = a, b
            r = DIL[g]
            nb = NB // r
            tiles = [(c, n) for c in range(r) for n in range(nb)]
            nt_ = len(tiles)
            dst = [dict() for _ in range(nt_)]

            def sl_(c, nn):
                st0 = c + r * 128 * nn
                return slice(st0, st0 + 127 * r + 1, r)

            def D0(t):
                c, n = tiles[t]
                qs = sl_(c, n)
                ps, B_ps = p1.next()
                dst[t]["ps"] = (ps, B_ps)
                if n > 0:
                    ks = sl_(c, n - 1)
                    P.op("pe", lambda e: e.matmul(ps[:, 0:128], lhsT=k[:, ks], rhs=q[:, qs], start=True, stop=False), reads=[B_hs], writes=[B_ps])
                    P.op("pe", lambda e: e.matmul(ps[:, 0:128], lhsT=cbf[:, C_IDENT:C_IDENT + 128], rhs=cbf[:, C_NEGD:C_NEGD + 128],
                                                  start=False, stop=True), reads=[B_c], writes=[B_ps])
                P.op("pe", lambda e: e.matmul(ps[:, 128:256], lhsT=k[:, qs], rhs=q[:, qs], start=True, stop=False), reads=[B_hs], writes=[B_ps])
                P.op("pe", lambda e: e.matmul(ps[:, 128:256], lhsT=cbf[:, C_IDENT:C_IDENT + 128], rhs=cbf[:, C_NEGD + 128:C_NEGD + 256],
                                              start=False, stop=True), reads=[B_c], writes=[B_ps])

            def D1(t):
                c, n = tiles[t]
                ps, B_ps = dst[t]["ps"]
                lo = 0 if n > 0 else 128
                pt, B_pt = ar.next()
                dst[t]["pt"] = (pt, B_pt)
                P.op("act", lambda e: e.activation(out=pt[:, lo:256], in_=ps[:, lo:256], func=AF.Exp), reads=[B_ps], writes=[B_pt])

            def D2(t):
                c, n = tiles[t]
                pt, B_pt = dst[t]["pt"]
                o_ps, B_o = po.next()
                d_ps, B_d = pd.next()
                dst[t]["o"] = (o_ps, B_o, d_ps, B_d)
                ti = c * nb + n
                if n > 0:
                    P.op("pe", lambda e: e.matmul(o_ps[:, 0:128], lhsT=v[:, ti - 1, :], rhs=pt[:, 0:128], start=True, stop=False),
                         reads=[B_hs, B_pt], writes=[B_o])
                P.op("pe", lambda e: e.matmul(o_ps[:, 0:128], lhsT=v[:, ti, :], rhs=pt[:, 128:256], start=(n == 0), stop=True),
                     reads=[B_hs, B_pt], writes=[B_o])
                if n > 0:
                    P.op("pe", lambda e: e.matmul(d_ps[:, 0:128], lhsT=cbf[:, C_ONES:C_ONES + 128], rhs=pt[:, 0:128], start=True, stop=False),
                         reads=[B_c, B_pt], writes=[B_d])
                P.op("pe", lambda e: e.matmul(d_ps[:, 0:128], lhsT=cbf[:, C_ONES:C_ONES + 128], rhs=pt[:, 128:256], start=(n == 0), stop=True),
                     reads=[B_c, B_pt], writes=[B_d])

            def D3(t):
                c, n = tiles[t]
                qs = sl_(c, n)
                o_ps, B_o, d_ps, B_d = dst[t]["o"]
                if g == 0:
                    P.op("dve", lambda e: e.tensor_copy(out=accO[:, qs], in_=o_ps[:, 0:128]), reads=[B_o], writes=[B_accO])
                    P.op("dve", lambda e: e.tensor_copy(out=accD[:, qs], in_=d_ps[:, 0:128]), reads=[B_d], writes=[B_accD])
                else:
                    P.op("dve", lambda e: e.tensor_tensor(out=accO[:, qs], in0=o_ps[:, 0:128], in1=accO[:, qs], op=ALU.add),
                         reads=[B_o, B_accO], writes=[B_accO])
                    P.op("dve", lambda e: e.tensor_tensor(out=accD[:, qs], in0=d_ps[:, 0:128], in1=accD[:, qs], op=ALU.add),
                         reads=[B_d, B_accD], writes=[B_accD])
                dst[t].clear()

            dstages = [D0, D1, D2, D3]
            for tick in range(nt_ + 3):
                for si, fn in enumerate(dstages):
                    t = tick - si
                    if 0 <= t < nt_:
                        fn(t)
            if g == NG - 1:
                for j in range(S // 512):
                    rc, B_rc = rcp.next()
                    P.op("dve", lambda e, rc=rc, j=j: e.reciprocal(out=rc[:], in_=accD[:, j * 512:(j + 1) * 512]), reads=[B_accD], writes=[B_rc])
                    y, B_y = yo.next()
                    P.op("dve", lambda e, y=y, rc=rc, j=j: e.tensor_tensor(out=y[:], in0=accO[:, j * 512:(j + 1) * 512], in1=rc[:], op=ALU.mult),
                         reads=[B_accO, B_rc], writes=[B_y])
                    P.dma(st_eng, lambda e, y=y, j=j, sl=sl: e.dma_start(out=bY[sl * 128:(sl + 1) * 128, j * 512:(j + 1) * 512], in_=y[:]), reads=[B_y])
                head_done(sl, yo.bufs)
        else:
            h = a
            steps = []
            for qg_ in range(S // 512):
                for kb in range(4 * qg_ + 3, -1, -1):
                    steps.append((qg_, kb))
            n = len(steps)
            st = [dict() for _ in range(n)]
            cur_o = [None]

            def mask_mm(e, ps, o):
                return e.matmul(ps[:], lhsT=cbf[:, C_IDENT:C_IDENT + 128], rhs=cbf[:, C_NEGW + 384 - 128 * o:C_NEGW + 896 - 128 * o],
                                start=False, stop=True)

            def S0(i):
                qg_, kb = steps[i]
                o = kb - 4 * qg_
                ps, B_ps = p1.next()
                st[i]["p1"] = (ps, B_ps)
                P.op("pe", lambda e: e.matmul(ps[:], lhsT=k[:, kb * 128:(kb + 1) * 128], rhs=q[:, qg_ * 512:(qg_ + 1) * 512], start=True, stop=(o < 0)),
                     reads=[B_hs], writes=[B_ps])
                if o >= 0:
                    P.op("pe", lambda e: mask_mm(e, ps, o), reads=[B_c], writes=[B_ps])

            def S1(i):
                ps, B_ps = st[i]["p1"]
                ee, B_e = er.next()
                st[i]["e"] = (ee, B_e)
                P.op("act", lambda e: e.activation(out=ee[:], in_=ps[:], func=AF.Exp), reads=[B_ps], writes=[B_e])
                sp, B_sp = spr.next()
                st[i]["sp"] = (sp, B_sp)
                P.op("act", lambda e: e.activation(out=sp[:], in_=ee[:], func=AF.Ln, bias=1.0, scale=1.0), reads=[B_e], writes=[B_sp])

            def S2(i):
                qg_, kb = steps[i]
                first = (kb == 4 * qg_ + 3)
                sp, B_sp = st[i]["sp"]
                ps, B_ps = p2.next()
                st[i]["p2"] = (ps, B_ps)
                if not first:
                    lb, B_lb = st[i]["lb"]
                    P.op("pe", lambda e: e.matmul(ps[:], lhsT=cbf[:, C_NONES:C_NONES + 128], rhs=lb[:], start=True, stop=False),
                         reads=[B_c, B_lb], writes=[B_ps])
                P.op("pe", lambda e: e.matmul(ps[:], lhsT=cbf[:, C_UNEG:C_UNEG + 128], rhs=sp[:], start=first, stop=True),
                     reads=[B_c, B_sp], writes=[B_ps])
                if kb > 0:
                    lb2, B_lb2 = lbr.next()
                    if first:
                        P.op("dve", lambda e: e.tensor_copy(out=lb2[:], in_=sp[:]), reads=[B_sp], writes=[B_lb2])
                        P.op("dve", lambda e: e.tensor_copy(out=lsum[:], in_=sp[:]), reads=[B_sp], writes=[B_lsum])
                    else:
                        P.op("dve", lambda e: e.tensor_tensor(out=lb2[:], in0=lsum[:], in1=sp[:], op=ALU.add), reads=[B_sp, B_lsum], writes=[B_lb2])
                        if kb > 1:
                            P.op("dve", lambda e: e.tensor_tensor(out=lsum[:], in0=lsum[:], in1=sp[:], op=ALU.add), reads=[B_sp, B_lsum], writes=[B_lsum])
                    st[i + 1]["lb"] = (lb2, B_lb2)

            def S3(i):
                ps, B_ps = st[i]["p2"]
                ee, B_e = st[i]["e"]
                ec, B_ec = ecr.next()
                P.op("act", lambda e: e.activation(out=ec[:], in_=ps[:], func=AF.Exp), reads=[B_ps], writes=[B_ec])
                aa, B_a = ar.next()
                st[i]["a"] = (aa, B_a)
                P.op("dve", lambda e: e.tensor_tensor(out=aa[:], in0=ee[:], in1=ec[:], op=ALU.mult), reads=[B_e, B_ec], writes=[B_a])

            def S4(i):
                qg_, kb = steps[i]
                first = (kb == 4 * qg_ + 3)
                aa, B_a = st[i]["a"]
                if first:
                    cur_o[0] = po.next()
                o_ps, B_o = cur_o[0]
                P.op("pe", lambda e: e.matmul(o_ps[:], lhsT=v[:, kb, :], rhs=aa[:], start=first, stop=(kb == 0)),
                     reads=[B_hs, B_a], writes=[B_o])
                if kb == 0:
                    y, B_y = yo.next()
                    P.op("dve", lambda e: e.tensor_copy(out=y[:], in_=o_ps[:]), reads=[B_o], writes=[B_y])
                    P.dma(st_eng, lambda e: e.dma_start(out=bY[256 + h * 128:256 + (h + 1) * 128, qg_ * 512:(qg_ + 1) * 512], in_=y[:]), reads=[B_y])
                st[i].clear()

            stages = [S0, S1, S2, S3, S4]
            for tick in range(n + 4):
                for si, fn in enumerate(stages):
                    i = tick - si
                    if 0 <= i < n:
                        fn(i)
            head_done(2 + h, yo.bufs)

    load_head(0)
    for hi_, (kind, a, b) in enumerate(heads):
        if hi_ + 1 < len(heads):
            load_head(hi_ + 1)
        q, k, v, B_hs = HS[hi_ % 2]
        run_head(hi_, kind, a, b, q, k, v, B_hs)


def phase_C(K, TOK, xT, xT_out, bYg, yloc, gT, wpd, wps, wpo, wp1, wp2, n2g, st_eng="pool"):
    P = K.P
    nt = TOK // T
    u = K.uid
    K.uid += 1
    wring = K.ring("wringC", [128, 8192], BF16, 3)
    ws = WStream(K, wring, 2)
    for t in range(nt):
        ws.add(wpd, 0, 0)
        ws.add(wps, 0, 0)
        ws.add(wps, 0, 1)
        for cb in range(4):
            ws.add(wpo, 0, cb)
        for half in range(2):
            for cb in range(8):
                ws.add(wp1, 0, half * 8 + cb)
            for cb in range(8):
                ws.add(wp2, half, cb)
    B_yloc = P.buf("yloc")
    for rk in range(2):
        P.dma("sp", lambda e, rk=rk: e.dma_start(out=yloc[rk * 768:(rk + 1) * 768, :].rearrange("(c x) t -> c x t", x=128),
                                                 in_=K.dynview(e, "bYg", lambda hf: bYg[:, :, :, bass.DynSlice(hf * TOK, TOK)])[:, rk]), writes=[B_yloc])
    par = K.sb("parC%d" % u, [128, 16], F32); B_par = P.buf("parC")
    P.dma("sp", lambda e: e.dma_start(out=par[:], in_=n2g), writes=[B_par])
    x = K.sb("xC%d" % u, [128, 16, T], F32)
    B_x = [P.buf("xC%d" % i) for i in range(16)]
    yd = K.sb("ydC%d" % u, [128, 4, T], BF16); B_yd = P.buf("yd")
    ys = K.sb("ysC%d" % u, [128, 8, T], BF16); B_ys = P.buf("ys")
    gr = K.ring("gC%d" % u, [128, 2, T], BF16, 3)
    mixed = K.sb("mixC%d" % u, [128, 16, T], BF16)
    B_mix = [P.buf("mix%d" % i) for i in range(16)]
    h2 = K.sb("h2C%d" % u, [128, 16, T], BF16)
    B_h2 = [P.buf("h2%d" % i) for i in range(16)]
    fT = K.sb("fC%d" % u, [128, 32, T], BF16)
    B_f = [P.buf("f%d" % i) for i in range(32)]
    acc = K.ring("accC%d" % u, [128, T], F32, 4, psum=True)
    ss_ps = K.ps("ssC%d" % u, [128, T]); B_ss = P.buf("ssC")
    t1r = K.ring("t1C%d" % u, [128, T], F32, 2)
    t2r = K.ring("t2C%d" % u, [128, T], F32, 2)
    sqr = K.ring("sqC%d" % u, [128, T], BF16, 3)
    rstd = K.sb("rstdC%d" % u, [128, T], F32); B_rstd = P.buf("rstdC")
    lnt = K.sb("lntC%d" % u, [128, T], F32); B_lnt = P.buf("lntC")
    cbf = K.cbf
    B_c = K.B_c
    B_out = P.buf("xout")
    wi = 0
    for t in range(nt):
        t0 = t * T
        for kc in range(16):
            P.dma("sp", lambda e, kc=kc, t0=t0: e.dma_start(out=x[:, kc, :], in_=xT[kc * 128:(kc + 1) * 128, t0:t0 + T]), writes=[B_x[kc]])
        for rk in range(2):
            P.dma("sp", lambda e, t0=t0, rk=rk: e.dma_start(
                out=yd[:, 2 * rk:2 * rk + 2, :],
                in_=yloc[rk * 768:rk * 768 + 256, t0:t0 + T].rearrange("(kc p) t -> p kc t", p=128)), reads=[B_yloc], writes=[B_yd])
            P.dma("sp", lambda e, t0=t0, rk=rk: e.dma_start(
                out=ys[:, 4 * rk:4 * rk + 4, :],
                in_=yloc[rk * 768 + 256:rk * 768 + 768, t0:t0 + T].rearrange("(kc p) t -> p kc t", p=128)), reads=[B_yloc], writes=[B_ys])
        wd_t, B_wd = ws.get(wi, 2); wi += 1
        ws_t = []
        for i in range(2):
            ws_t.append(ws.get(wi, 1 - i)); wi += 1
        for c in range(16):
            gt, B_g = gr.next()
            P.dma("sp", lambda e, gt=gt, c=c, t0=t0: e.dma_start(out=gt[:, 0, :], in_=gT[c, :, t0:t0 + T]), writes=[B_g])
            P.dma("sp", lambda e, gt=gt, c=c, t0=t0: e.dma_start(out=gt[:, 1, :], in_=gT[16 + c, :, t0:t0 + T]), writes=[B_g])
            pa, B_pa = acc.next()
            for kc in range(4):
                P.op("pe", lambda e, pa=pa, kc=kc, c=c, wd_t=wd_t: e.matmul(pa[:], lhsT=wd_t[:, kc, c * 128:(c + 1) * 128], rhs=yd[:, kc, :], start=(kc == 0), stop=(kc == 3)),
                     reads=[B_wd, B_yd], writes=[B_pa])
            pb, B_pb = acc.next()
            wst, B_wst = ws_t[c // 8]
            cc = c % 8
            for kc in range(8):
                P.op("pe", lambda e, pb=pb, kc=kc, cc=cc, wst=wst: e.matmul(pb[:], lhsT=wst[:, kc, cc * 128:(cc + 1) * 128], rhs=ys[:, kc, :], start=(kc == 0), stop=(kc == 7)),
                     reads=[B_wst, B_ys], writes=[B_pb])
            t1, B_t1 = t1r.next()
            P.op("dve", lambda e, t1=t1, pa=pa, gt=gt: e.tensor_tensor(out=t1[:], in0=pa[:], in1=gt[:, 0, :], op=ALU.mult), reads=[B_pa, B_g], writes=[B_t1])
            t2, B_t2 = t2r.next()
            P.op("dve", lambda e, t2=t2, pb=pb, gt=gt: e.tensor_tensor(out=t2[:], in0=pb[:], in1=gt[:, 1, :], op=ALU.mult), reads=[B_pb, B_g], writes=[B_t2])
            P.op("pool", lambda e, t1=t1, t2=t2, c=c: e.tensor_tensor(out=mixed[:, c, :], in0=t1[:], in1=t2[:], op=ALU.add), reads=[B_t1, B_t2], writes=[B_mix[c]])
        for cb in range(4):
            wt, B_w = ws.get(wi); wi += 1
            for ci in range(4):
                c = cb * 4 + ci
                pa, B_pa = acc.next()
                for kc in range(16):
                    P.op("pe", lambda e, pa=pa, kc=kc, ci=ci, wt=wt: e.matmul(pa[:], lhsT=wt[:, kc, ci * 128:(ci + 1) * 128], rhs=mixed[:, kc, :], start=(kc == 0), stop=(kc == 15)),
                         reads=[B_w, B_mix[kc]], writes=[B_pa])
                P.op("dve", lambda e, pa=pa, c=c: e.tensor_tensor(out=x[:, c, :], in0=pa[:], in1=x[:, c, :], op=ALU.add), reads=[B_pa, B_x[c]], writes=[B_x[c]])
                sq, B_sq = sqr.next()
                P.op("act", lambda e, sq=sq, c=c: e.activation(out=sq[:], in_=x[:, c, :], func=AF.Square), reads=[B_x[c]], writes=[B_sq])
                P.op("pool", lambda e, c=c: e.tensor_scalar(out=h2[:, c, :], in0=x[:, c, :], scalar1=par[:, c:c + 1], scalar2=None, op0=ALU.mult),
                     reads=[B_x[c], B_par], writes=[B_h2[c]])
                P.op("pe", lambda e, sq=sq, c=c: e.matmul(ss_ps[:], lhsT=cbf[:, C_ONES:C_ONES + 128], rhs=sq[:], start=(c == 0), stop=(c == 15)),
                     reads=[B_sq, B_c], writes=[B_ss])
        rstd_from_ss(K, ss_ps[:], B_ss, rstd[:], B_rstd, lnt[:], B_lnt, 1.0 / D)
        for half in range(2):
            for cb in range(8):
                wt, B_w = ws.get(wi); wi += 1
                for ci in range(4):
                    fc = cb * 4 + ci
                    pa, B_pa = acc.next()
                    for kc in range(16):
                        P.op("pe", lambda e, pa=pa, kc=kc, ci=ci, wt=wt: e.matmul(pa[:], lhsT=wt[:, kc, ci * 128:(ci + 1) * 128], rhs=h2[:, kc, :], start=(kc == 0), stop=(kc == 15)),
                             reads=[B_w, B_h2[kc]], writes=[B_pa])
                    t1, B_t1 = t1r.next()
                    P.op("dve", lambda e, t1=t1, pa=pa: e.scalar_tensor_tensor(out=t1[:], in0=pa[:], scalar=0.0, in1=rstd[:], op0=ALU.max, op1=ALU.mult),
                         reads=[B_pa, B_rstd], writes=[B_t1])
                    P.op("act", lambda e, t1=t1, fc=fc: e.activation(out=fT[:, fc, :], in_=t1[:], func=AF.Square), reads=[B_t1], writes=[B_f[fc]])
            for cb in range(8):
                wt, B_w = ws.get(wi); wi += 1
                wt2 = wt
                for ci in range(2):
                    c = cb * 2 + ci
                    pa, B_pa = acc.next()
                    for kc in range(32):
                        P.op("pe", lambda e, pa=pa, kc=kc, ci=ci, wt2=wt2: e.matmul(pa[:], lhsT=wt2[:, kc, ci * 128:(ci + 1) * 128], rhs=fT[:, kc, :], start=(kc == 0), stop=(kc == 31)),
                             reads=[B_w, B_f[kc]], writes=[B_pa])
                    P.op("dve", lambda e, pa=pa, c=c: e.tensor_tensor(out=x[:, c, :], in0=pa[:], in1=x[:, c, :], op=ALU.add), reads=[B_pa, B_x[c]], writes=[B_x[c]])
                    if half == 1:
                        P.dma(st_eng, lambda e, c=c, t0=t0: e.dma_start(out=xT_out[c * 128:(c + 1) * 128, t0:t0 + T], in_=x[:, c, :]),
                              reads=[B_x[c]], writes=[B_out], sem_of=B_x[c])


def make_wprep_in(K, l, w):
    return {"in": WPrep(K, "in%d" % l, w["w_in"][l], D, NIN, 16, 512, ngroups=(4 if l == 0 else 1))}


def make_wpreps_c(K, l, w):
    return {
        "ud": WPrep(K, "ud%d" % l, w["w_ud"][l], 512, D, 4, 2048),
        "us": WPrep(K, "us%d" % l, w["w_us"][l], 1024, D, 8, 1024),
        "o": WPrep(K, "o%d" % l, w["w_o"][l], D, D, 16, 512),
        "f1": WPrep(K, "f1%d" % l, w["w1"][l], D, DFF, 16, 512),
        "f2": WPrep(K, "f2%d" % l, w["w2"][l], DFF, D, 32, 256),
    }


def build_fused(S):
    TOK = S // 2
    nc = bass.Bass("TRN2", target_bir_lowering=False)
    with ExitStack() as es:
        K = KB(nc, es)
        P = K.P
        xT = K.dram("xT", [D, TOK], F32, "ExternalInput")
        w = {
            "w_in": K.dram("w_in", [DEPTH, D, NIN], F32, "ExternalInput"),
            "w_ud": K.dram("w_ud", [DEPTH, 512, D], F32, "ExternalInput"),
            "w_us": K.dram("w_us", [DEPTH, 1024, D], F32, "ExternalInput"),
            "w_o": K.dram("w_o", [DEPTH, D, D], F32, "ExternalInput"),
            "w1": K.dram("w1", [DEPTH, D, DFF], F32, "ExternalInput"),
            "w2": K.dram("w2", [DEPTH, DFF, D], F32, "ExternalInput"),
        }
        n1g = K.dram("n1g", [DEPTH, 128, 16], F32, "ExternalInput")
        n2g = K.dram("n2g", [DEPTH, 128, 16], F32, "ExternalInput")
        qg = K.dram("qg", [DEPTH, 128, 3], F32, "ExternalInput")
        kg = K.dram("kg", [DEPTH, 128, 3], F32, "ExternalInput")
        gb = K.dram("gb", [DEPTH, 128, 32], F32, "ExternalInput")
        cosT = K.dram("cosT", [128, TOK], F32, "ExternalInput")
        sinT = K.dram("sinT", [128, TOK], F32, "ExternalInput")
        cbf = K.dram("cbf", [128, NCBF], BF16, "ExternalInput")
        cf32 = K.dram("cf32", [128, NCF32], F32, "ExternalInput")
        xo = K.dram("xT_out", [D, TOK], F32, "ExternalOutput")
        xmid = K.dram("xT_mid", [D, TOK], F32)
        NT = TOK // T
        aT = K.dram("aT", [NT, 5120, T], BF16)
        aV = K.dram("aV", [TOK, 2560], BF16)
        gT = K.dram("gT", [32, 128, TOK], BF16)
        aTg = K.dram("aTg", [NT, 4, 2, 1280, T], BF16)
        aVg = K.dram("aVg", [TOK // 256, 2, 256, 2560], BF16)
        bY = K.dram("bY", [768, S], BF16)
        bYg = K.dram("bYg", [6, 2, 128, S], BF16)
        vloc = K.dram("vloc", [S, 1280], BF16)
        tloc = K.dram("tloc", [2, 2560, TOK], BF16)
        yloc = K.dram("yloc", [2 * 768, TOK], BF16)
        groups = [[0, 1], [2, 3], [4, 5], [6, 7]]
        B_cc = P.buf("cc")

        def ag(src, dst, after):
            P.dma("pool", lambda e: e.collective_compute("AllGather", ALU.bypass, replica_groups=groups, ins=[src], outs=[dst]),
                  writes=[B_cc], inc=1, after=after)

        def tile_done(t, stage_bufs):
            for j in range(4):
                ag(aT[t, j * 1280:(j + 1) * 1280, :], aTg[t, j].rearrange("r x c -> (r x) c"), stage_bufs)
            for c in range(2 * t, 2 * t + 2):
                ag(aV[c * 256:(c + 1) * 256, :], aVg[c].rearrange("r x d -> (r x) d"), stage_bufs)

        def head_done(c, stage_bufs):
            ag(bY[c * 128:(c + 1) * 128, :], bYg[c].rearrange("r x t -> (r x) t"), stage_bufs)

        K.load_consts(cbf, cf32)
        wps = make_wprep_in(K, 0, w)
        for l in range(DEPTH):
            x_in = xT if l == 0 else xmid
            x_out = xmid if l == 0 else xo
            phase_A(K, TOK, x_in, n1g[l], wps["in"], qg[l], kg[l], gb[l], cosT, sinT, aT, aV, gT, tile_done)
            K.new_phase()
            wps.update(make_wpreps_c(K, l, w))
            if l + 1 < DEPTH:
                wps_next = make_wprep_in(K, l + 1, w)
            phase_B(K, S, aTg, aVg, bY, vloc, tloc, head_done)
            K.new_phase()
            phase_C(K, TOK, x_in, x_out, bYg, yloc, gT, wps["ud"], wps["us"], wps["o"], wps["f1"], wps["f2"], n2g[l])
            K.new_phase()
            if l + 1 < DEPTH:
                wps = wps_next
        P.run(nc)
    return nc


_CACHE = {}


def fm(v):
    return np.ascontiguousarray(v.reshape(v.shape[0], -1, 128).transpose(0, 2, 1))


def run_model(x, norm1_g, w_in, q_norm_g, k_norm_g, w_up_dil, w_up_sb, gate_b, w_out, norm2_g, w_ff1, w_ff2):
    Bn, S, _ = x.shape
    TOK = S // 2
    cbf, cf32 = make_consts()
    cosT, sinT = rope_tables(S)
    cores = list(range(NCORES))
    if S not in _CACHE:
        _CACHE[S] = build_fused(S)
    nc = _CACHE[S]
    shared = {
        "w_in": np.ascontiguousarray(w_in), "w_ud": np.ascontiguousarray(w_up_dil), "w_us": np.ascontiguousarray(w_up_sb),
        "w_o": np.ascontiguousarray(w_out), "w1": np.ascontiguousarray(w_ff1), "w2": np.ascontiguousarray(w_ff2),
        "n1g": fm(norm1_g), "n2g": fm(norm2_g),
        "qg": np.ascontiguousarray(q_norm_g.transpose(0, 2, 1)), "kg": np.ascontiguousarray(k_norm_g.transpose(0, 2, 1)),
        "gb": fm(gate_b.reshape(DEPTH, -1)), "cbf": cbf, "cf32": cf32,
    }
    maps = []
    for c in cores:
        b, hf = c // 2, c % 2
        m = dict(shared)
        m["xT"] = np.ascontiguousarray(x[b, hf * TOK:(hf + 1) * TOK, :].T)
        m["cosT"] = np.ascontiguousarray(cosT[:, hf * TOK:(hf + 1) * TOK])
        m["sinT"] = np.ascontiguousarray(sinT[:, hf * TOK:(hf + 1) * TOK])
        maps.append(m)
    res = run_bass_kernel_spmd(nc, maps, core_ids=cores).results
    out = np.empty((Bn, S, D), np.float32)
    for c in cores:
        out[c // 2, (c % 2) * TOK:(c % 2 + 1) * TOK, :] = res[c]["xT_out"].T
    return out


def kernel(x, norm1_g, w_in, q_norm_g, k_norm_g, w_up_dil, w_up_sb, gate_b, w_out, norm2_g, w_ff1, w_ff2):
    a = [np.asarray(v, dtype=np.float32) for v in
         (x, norm1_g, w_in, q_norm_g, k_norm_g, w_up_dil, w_up_sb, gate_b, w_out, norm2_g, w_ff1, w_ff2)]
    return run_model(*a)
```
